# Optimizing a Trainium2 kernel written in Bass

```python
import math
import jax
import jax.numpy as jnp
from jax import lax
import numpy as np

D_MODEL = 1024
BATCH = 4
SEQ = 8192
DEPTH = 2

BLOCK = 128
N_BRANCH = 4
BRANCH_WIDTH = D_MODEL // 2

SWA_HEAD_DIM = 64
SWA_HEADS = BRANCH_WIDTH // SWA_HEAD_DIM
SWA_KV_HEADS = SWA_HEADS // 4
SWA_WINDOW = 128

SB_HEAD_DIM = 64
SB_HEADS = BRANCH_WIDTH // SB_HEAD_DIM

LRU_WIDTH = BRANCH_WIDTH
LRU_BLOCKS = 8
CONV_WIDTH = 4
LRU_C = 8.0

MEM_LEN = 256
MEM_HEADS = 4
MEM_HEAD_DIM = BRANCH_WIDTH // MEM_HEADS

EPS = 1e-6

SPLIT_SIZES = (
    SWA_HEADS * SWA_HEAD_DIM, SWA_KV_HEADS * SWA_HEAD_DIM, SWA_KV_HEADS * SWA_HEAD_DIM, BRANCH_WIDTH,
    BRANCH_WIDTH, BRANCH_WIDTH, BRANCH_WIDTH, BRANCH_WIDTH,
    LRU_WIDTH, LRU_WIDTH,
    BRANCH_WIDTH, BRANCH_WIDTH,
    N_BRANCH * D_MODEL,
)
IN_WIDTH = sum(SPLIT_SIZES)

kernel_name = "hybrid_gated_swa_stickbreak_rglru_mem"


def rms_norm(x, gain):
    xf = x.astype(jnp.float32)
    y = xf * lax.rsqrt(jnp.mean(xf * xf, axis=-1, keepdims=True) + EPS)
    return (y * gain.astype(jnp.float32)).astype(x.dtype)


def sliding_window_attention(q, k, v, sinks):
    b, s, h, d = q.shape
    kvh = k.shape[2]
    g = h // kvh
    nb = s // BLOCK
    qb = q.reshape(b, nb, BLOCK, kvh, g, d)

    def with_prev(t):
        tb = t.reshape(b, nb, BLOCK, kvh, d)
        prev = jnp.pad(tb[:, :-1], ((0, 0), (1, 0), (0, 0), (0, 0), (0, 0)))
        return jnp.concatenate([prev, tb], axis=2)

    kb, vb = with_prev(k), with_prev(v)
    scores = jnp.einsum('bnqkgd,bnskd->bkgnqs', qb, kb).astype(jnp.float32) * (d ** -0.5)
    qpos = jnp.arange(BLOCK)[:, None]
    kpos = jnp.arange(2 * BLOCK)[None, :] - BLOCK
    diff = qpos - kpos
    band = (diff >= 0) & (diff < SWA_WINDOW)
    key_abs = jnp.arange(nb)[:, None, None] * BLOCK + kpos[None]
    valid = band[None] & (key_abs >= 0)
    scores = jnp.where(valid, scores, -jnp.inf)
    sink = sinks.astype(jnp.float32).reshape(kvh, g)[None, :, :, None, None, None]
    m = jnp.maximum(jnp.max(scores, axis=-1, keepdims=True), sink)
    p = jnp.exp(scores - m)
    probs = p / (jnp.sum(p, axis=-1, keepdims=True) + jnp.exp(sink - m))
    out = jnp.einsum('bkgnqs,bnskd->bnqkgd', probs.astype(v.dtype), vb)
    return out.reshape(b, s, h, d)


def stick_breaking_attention(q, k, v):
    b, s, h, d = q.shape
    nb = s // BLOCK
    qb = q.reshape(b, nb, BLOCK, h, d).transpose(1, 0, 3, 2, 4)
    kpos = jnp.arange(s)
    scale = d ** -0.5

    def block(args):
        q_blk, n = args
        z = jnp.einsum('bhqd,bshd->bhqs', q_blk, k).astype(jnp.float32) * scale
        qpos = n * BLOCK + jnp.arange(BLOCK)
        causal = kpos[None, :] < qpos[:, None]
        log_keep = jnp.where(causal, jax.nn.log_sigmoid(-z), 0.0)
        between = lax.cumsum(log_keep, axis=3, reverse=True) - log_keep
        weights = jnp.where(causal, jnp.exp(jax.nn.log_sigmoid(z) + between), 0.0)
        return jnp.einsum('bhqs,bshd->bqhd', weights.astype(v.dtype), v)

    out = lax.map(block, (qb, jnp.arange(nb)))
    return out.transpose(1, 0, 2, 3, 4).reshape(b, s, h, d)


def causal_depthwise_conv(x, w, bias):
    c = x.shape[-1]
    y = lax.conv_general_dilated(x, w[:, None, :].astype(x.dtype), window_strides=(1,),
                                 padding=[(CONV_WIDTH - 1, 0)],
                                 dimension_numbers=('NWC', 'WIO', 'NWC'),
                                 feature_group_count=c)
    return y + bias


def rg_lru(x, w_a, b_a, w_x, b_x, lam):
    b, s, c = x.shape
    xb = x.reshape(b, s, LRU_BLOCKS, c // LRU_BLOCKS)
    r = jax.nn.sigmoid(jnp.einsum('bsnc,ncd->bsnd', xb, w_a).reshape(b, s, c) + b_a)
    i = jax.nn.sigmoid(jnp.einsum('bsnc,ncd->bsnd', xb, w_x).reshape(b, s, c) + b_x)
    log_a = -LRU_C * r.astype(jnp.float32) * jax.nn.softplus(-lam.astype(jnp.float32))
    a = jnp.exp(log_a)
    inp = jnp.sqrt(-jnp.expm1(2.0 * log_a)) * (i * x).astype(jnp.float32)

    def combine(left, right):
        a1, b1 = left
        a2, b2 = right
        return a1 * a2, a2 * b1 + b2

    _, h = lax.associative_scan(combine, (a, inp), axis=1)
    return h.astype(x.dtype)


def memory_attention(q, mk, mv):
    d = q.shape[-1]
    scores = jnp.einsum('bshd,bmhd->bhsm', q, mk).astype(jnp.float32) * (d ** -0.5)
    p = jax.nn.softmax(scores, axis=-1)
    return jnp.einsum('bhsm,bmhd->bshd', p.astype(mv.dtype), mv)


def hybrid_layer(x, mem, norm_gain, w_in, swa_q_gain, swa_k_gain, swa_sinks, conv_w, conv_b,
                 lru_w_a, lru_b_a, lru_w_x, lru_b_x, lru_lambda, mem_norm_gain, w_mem_kv,
                 mem_q_gain, mem_k_gain, w_branch, w_out):
    b, s, _ = x.shape
    u = rms_norm(x, norm_gain)
    proj = u @ w_in
    split_points = [int(p) for p in np.cumsum(SPLIT_SIZES)[:-1]]
    (a_q, a_k, a_v, a_g, b_q, b_k, b_v, b_g, c_x, c_g, m_q, m_g, merge) = jnp.split(proj, split_points, axis=-1)

    qa = rms_norm(a_q.reshape(b, s, SWA_HEADS, SWA_HEAD_DIM), swa_q_gain)
    ka = rms_norm(a_k.reshape(b, s, SWA_KV_HEADS, SWA_HEAD_DIM), swa_k_gain)
    va = a_v.reshape(b, s, SWA_KV_HEADS, SWA_HEAD_DIM)
    ya = sliding_window_attention(qa, ka, va, swa_sinks).reshape(b, s, BRANCH_WIDTH) * jax.nn.silu(a_g)

    yb = stick_breaking_attention(b_q.reshape(b, s, SB_HEADS, SB_HEAD_DIM),
                                  b_k.reshape(b, s, SB_HEADS, SB_HEAD_DIM),
                                  b_v.reshape(b, s, SB_HEADS, SB_HEAD_DIM)).reshape(b, s, BRANCH_WIDTH)
    yb = yb * jax.nn.silu(b_g)

    xc = causal_depthwise_conv(c_x, conv_w, conv_b)
    yc = rg_lru(xc, lru_w_a, lru_b_a, lru_w_x, lru_b_x, lru_lambda) * jax.nn.silu(c_g)

    mlen = mem.shape[1]
    mkv = rms_norm(mem, mem_norm_gain) @ w_mem_kv
    mk, mv = jnp.split(mkv, 2, axis=-1)
    mk = rms_norm(mk.reshape(b, mlen, MEM_HEADS, MEM_HEAD_DIM), mem_k_gain)
    mv = mv.reshape(b, mlen, MEM_HEADS, MEM_HEAD_DIM)
    qm = rms_norm(m_q.reshape(b, s, MEM_HEADS, MEM_HEAD_DIM), mem_q_gain)
    ym = memory_attention(qm, mk, mv).reshape(b, s, BRANCH_WIDTH) * jax.nn.silu(m_g)

    branches = jnp.stack([ya, yb, yc, ym], axis=2)
    up = jnp.einsum('bsnw,nwd->bsnd', branches, w_branch)
    gates = jax.nn.sigmoid(merge.reshape(b, s, N_BRANCH, D_MODEL))
    mixed = jnp.sum(gates * up, axis=2)
    return x + mixed @ w_out


def setup_inputs(seed: int = 0) -> dict:
    key = jax.random.key(seed)
    ks = jax.random.split(key, 24)
    f32 = jnp.float32
    hb = LRU_WIDTH // LRU_BLOCKS
    nrm = lambda k, shape, scale: jax.random.normal(k, shape, f32) * scale
    u = jax.random.uniform(ks[12], (DEPTH, LRU_WIDTH), f32, 0.9, 0.999)
    sig = u ** (1.0 / LRU_C)
    lru_lambda = jnp.log(sig) - jnp.log1p(-sig)
    return {
        'x': nrm(ks[0], (BATCH, SEQ, D_MODEL), 1.0),
        'mem': nrm(ks[1], (BATCH, MEM_LEN, D_MODEL), 1.0),
        'norm_gain': 1.0 + nrm(ks[2], (DEPTH, D_MODEL), 0.02),
        'w_in': nrm(ks[3], (DEPTH, D_MODEL, IN_WIDTH), D_MODEL ** -0.5),
        'swa_q_gain': 1.0 + nrm(ks[4], (DEPTH, SWA_HEAD_DIM), 0.02),
        'swa_k_gain': 1.0 + nrm(ks[5], (DEPTH, SWA_HEAD_DIM), 0.02),
        'swa_sinks': nrm(ks[6], (DEPTH, SWA_HEADS), 0.5),
        'conv_w': nrm(ks[7], (DEPTH, CONV_WIDTH, LRU_WIDTH), CONV_WIDTH ** -0.5),
        'conv_b': nrm(ks[8], (DEPTH, LRU_WIDTH), 0.01),
        'lru_w_a': nrm(ks[9], (DEPTH, LRU_BLOCKS, hb, hb), hb ** -0.5),
        'lru_b_a': nrm(ks[10], (DEPTH, LRU_WIDTH), 0.01),
        'lru_w_x': nrm(ks[11], (DEPTH, LRU_BLOCKS, hb, hb), hb ** -0.5),
        'lru_b_x': nrm(ks[13], (DEPTH, LRU_WIDTH), 0.01),
        'lru_lambda': lru_lambda,
        'mem_norm_gain': 1.0 + nrm(ks[14], (DEPTH, D_MODEL), 0.02),
        'w_mem_kv': nrm(ks[15], (DEPTH, D_MODEL, 2 * BRANCH_WIDTH), D_MODEL ** -0.5),
        'mem_q_gain': 1.0 + nrm(ks[16], (DEPTH, MEM_HEAD_DIM), 0.02),
        'mem_k_gain': 1.0 + nrm(ks[17], (DEPTH, MEM_HEAD_DIM), 0.02),
        'w_branch': nrm(ks[18], (DEPTH, N_BRANCH, BRANCH_WIDTH, D_MODEL), BRANCH_WIDTH ** -0.5),
        'w_out': nrm(ks[19], (DEPTH, D_MODEL, D_MODEL), D_MODEL ** -0.5),
    }


def reference(x, mem, norm_gain, w_in, swa_q_gain, swa_k_gain, swa_sinks, conv_w, conv_b,
              lru_w_a, lru_b_a, lru_w_x, lru_b_x, lru_lambda, mem_norm_gain, w_mem_kv,
              mem_q_gain, mem_k_gain, w_branch, w_out):
    for l in range(DEPTH):
        x = hybrid_layer(x, mem, norm_gain[l], w_in[l], swa_q_gain[l], swa_k_gain[l], swa_sinks[l],
                         conv_w[l], conv_b[l], lru_w_a[l], lru_b_a[l], lru_w_x[l], lru_b_x[l],
                         lru_lambda[l], mem_norm_gain[l], w_mem_kv[l], mem_q_gain[l], mem_k_gain[l],
                         w_branch[l], w_out[l])
    return x
```

```python
import numpy as np
from contextlib import ExitStack
import concourse.bass as bass
import concourse.mybir as mybir
from concourse.bass_utils import run_bass_kernel_spmd

F32 = mybir.dt.float32
BF16 = mybir.dt.bfloat16
AF = mybir.ActivationFunctionType
ALU = mybir.AluOpType
EPS = 1e-6
NEG = -30000.0


class Res:
    __slots__ = ("name", "w", "r")

    def __init__(self, name):
        self.name = name
        self.w = None
        self.r = {}


class TT:
    __slots__ = ("t", "res", "name")

    def __init__(self, t, name):
        self.t = t
        self.name = name
        self.res = Res(name)

    def __getitem__(self, k):
        return self.t[k]


class Prog:
    def __init__(self):
        self.streams = {"pe": [], "act": [], "dve": [], "pool": [], "sp": []}
        self.count = {}
        self.seen = {e: {} for e in self.streams}
        self.chans = []

    def op(self, eng, fn, R=(), W=(), chan=None, inc=None):
        dma = chan is not None
        ch = chan if dma else eng
        if ch not in self.count:
            self.count[ch] = 0
            self.chans.append(ch)
        need = {}

        def add(c, n):
            if need.get(c, 0) < n:
                need[c] = n
        for r in R:
            if r.w is not None:
                add(*r.w)
        for w in W:
            if w.w is not None:
                add(*w.w)
            for c, n in w.r.items():
                add(c, n)
        seen = self.seen[eng]
        waits = []
        for c, n in need.items():
            if c == "pe" and eng == "pe":
                continue
            if seen.get(c, 0) >= n:
                continue
            seen[c] = n
            waits.append((c, n))
        if inc is None:
            inc = 16 if dma else 1
        self.count[ch] += inc
        my = self.count[ch]
        for r in R:
            if r.r.get(ch, 0) < my:
                r.r[ch] = my
        for w in W:
            w.w = (ch, my)
            w.r = {}
        self.streams[eng].append((waits, fn, ch, inc))

    def raw(self, eng, fn):
        self.streams[eng].append(([], fn, None, 0))

    def barrier(self):
        waits = [(c, n) for c, n in self.count.items() if n > 0]
        for eng in self.streams:
            self.streams[eng].append((list(waits), None, None, 0))
            for c, n in waits:
                if self.seen[eng].get(c, 0) < n:
                    self.seen[eng][c] = n

    def mm(self, out, lhsT, rhs, start, stop, R, W):
        self.op("pe", lambda e: e.matmul(out, lhsT, rhs, start=start, stop=stop), R, W)

    def tr(self, out, in_, ident, R, W):
        self.op("pe", lambda e: e.transpose(out, in_, ident), R, W)

    def act(self, out, in_, func, R, W, scale=1.0, bias=None, accum=None):
        kw = {}
        if bias is not None:
            kw["bias"] = bias
        if accum is not None:
            kw["accum_out"] = accum
        self.op("act", lambda e: e.activation(out=out, in_=in_, func=func, scale=scale, **kw), R, W)

    def tt(self, eng, out, a, b, op, R, W):
        self.op(eng, lambda e: e.tensor_tensor(out=out, in0=a, in1=b, op=op), R, W)

    def ts(self, eng, out, a, s1, s2, op0, op1, R, W):
        self.op(eng, lambda e: e.tensor_scalar(out=out, in0=a, scalar1=s1, scalar2=s2, op0=op0, op1=op1), R, W)

    def stt(self, eng, out, a, s, b, op0, op1, R, W):
        self.op(eng, lambda e: e.scalar_tensor_tensor(out=out, in0=a, scalar=s, in1=b, op0=op0, op1=op1), R, W)

    def cp(self, eng, out, in_, R, W):
        if eng == "act":
            self.op("act", lambda e: e.activation(out=out, in_=in_, func=AF.Copy), R, W)
        else:
            self.op(eng, lambda e: e.tensor_copy(out=out, in_=in_), R, W)

    def memset(self, eng, ap, val, W):
        self.op(eng, lambda e: e.memset(ap, val), (), W)

    def recip(self, out, in_, R, W):
        self.op("dve", lambda e: e.reciprocal(out=out, in_=in_), R, W)

    def dma(self, q, out, in_, R, W, chan):
        self.op(q, lambda e: e.dma_start(out=out, in_=in_), R, W, chan=chan)


O_AQ, O_AK, O_AV, O_AG = 0, 512, 640, 768
O_BQ, O_BK, O_BV, O_BG = 1280, 1792, 2304, 2816
O_CX, O_CG, O_MQ, O_MG, O_MERGE = 3328, 3840, 4352, 4864, 5376
IN_W = 9472
NCV = 35
NRV = 8 + 128


def build(S, L, NCORES):
    RG = [[2 * k, 2 * k + 1] for k in range(NCORES // 2)]
    NT = S // 512
    NB = S // 128
    nc = bass.Bass("TRN2", target_bir_lowering=False)
    P = Prog()
    es = ExitStack()

    def din(name, shape, dt=F32):
        return nc.dram_tensor(name, shape, dt, kind="ExternalInput").ap()

    NTH = NT // 2
    x_in = din("x", [S, 1024])
    x_own = din("x_own", [S // 2, 1024])
    rk = nc.dram_tensor("rk", [1, 2], mybir.dt.int32, kind="ExternalInput").ap()
    mem_in = din("mem", [256, 1024])
    w_c = din("w_c", [L, 1024, 6144])
    w_sb = din("w_sb", [L, 1024, 768])
    w_mem = din("w_mem_kv", [L, 1024, 1024])
    w_br = din("w_branch", [L, 4, 512, 1024])
    w_out = din("w_out", [L, 1024, 1024])
    gain_bc = din("gain_bc", [L, 128, 1024])
    mgain_bc = din("mgain_bc", [L, 128, 1024])
    colv = din("colv", [L, 128, NCV])
    rowv = din("rowv", [L, 128, NRV])
    lruw = din("lruw", [L, 2, 4, 128, 128])
    consts = din("consts", [128, 8, 128])
    y_out = nc.dram_tensor("y", [S // 2, 1024], F32, kind="ExternalOutput").ap()

    def dscr(name, shape, dt=BF16):
        return nc.dram_tensor(name, shape, dt).ap()

    WSB = [dscr(f"WSB{l}", [128, 8, 768]) for l in range(L)]
    WC = [dscr(f"WC{l}", [128, 8 * 6144]) for l in range(L)]
    WBR = [dscr(f"WBR{l}", [128, 8, 4, 4, 128]) for l in range(L)]
    WOUT = [dscr(f"WOUT{l}", [128, 8 * 1024]) for l in range(L)]
    WMEM = [dscr(f"WMEM{l}", [128, 8, 1024]) for l in range(L)]
    UT = dscr("UT", [128, 8, S])
    QZ = dscr("QZ", [128, 4, S])
    KT = dscr("KT", [128, 2, S])
    VS = dscr("VS", [NB, 128, 256])
    NPC = max(1, NB // 32)
    NBP = NB // NPC
    YBH = [[dscr(f"YBH{l}_{q}", [NBP * 128, 256]) for q in range(NPC)] for l in range(L)]
    YBF = [[dscr(f"YBF{l}_{q}", [2 * NBP * 128, 256]) for q in range(NPC)] for l in range(L)]
    RYBF = [Res(f"rybf{l}") for l in range(L)]
    X1HL = [[dscr(f"X1H{q}_{t}", [512, 1024], F32) for t in range(NTH)] for q in range(2)]
    X1F = [dscr(f"X1F{t}", [1024, 1024], F32) for t in range(NTH)]
    YTSH = [dscr(f"YTSH{p}", [2 * 128 * 8, 512]) for p in range(NTH)]
    YTSF = [dscr(f"YTSF{p}", [2 * 2 * 128 * 8, 512]) for p in range(NTH)]
    RYTS = [Res(f"ryts{i}") for i in range(NT)]
    RYTSF = [Res(f"rytsf{p}") for p in range(NTH)]
    RWSB = [Res(f"rwsb{l}") for l in range(L)]
    RWC = [Res(f"rwc{l}") for l in range(L)]
    RWBR = [Res(f"rwbr{l}") for l in range(L)]
    RWOUT = [Res(f"rwout{l}") for l in range(L)]
    RWMEM = [Res(f"rwmem{l}") for l in range(L)]
    RUT = [Res(f"rut{i}") for i in range(NT)]
    RQZ = [Res(f"rqz{i}") for i in range(NT)]
    RKT = [Res(f"rkt{i}") for i in range(NT)]
    RVS = [Res(f"rvs{i}") for i in range(NT)]
    RYB = [Res(f"ryb{i}") for i in range(NT)]
    RX1HL = [[Res(f"rx1h{q}_{t}") for t in range(NTH)] for q in range(2)]
    RX1F = [Res(f"rx1f{t}") for t in range(NTH)]
    rkv = {}

    def load_rank(e):
        reg = e.alloc_register("rkreg")
        ins = e.reg_load(reg, rk[0:1, 0:1])
        rkv["r"] = e.snap(reg, min_val=0, max_val=1)
        return ins
    P.raw("sp", load_rank)

    def sb(name, shape, dt=F32):
        return TT(es.enter_context(nc.sbuf_tensor(name, shape, dt)), name)

    ARENA_W = 48 * 1024
    arena = es.enter_context(nc.sbuf_tensor("arena", [128, ARENA_W], F32))
    car = [0]

    def carve(name, shape, dt=F32):
        n = int(np.prod(shape[1:]))
        nw = n if dt == F32 else (n + 1) // 2
        nw = (nw + 7) // 8 * 8
        ap = arena[:, car[0]:car[0] + nw]
        car[0] += nw
        assert car[0] <= ARENA_W, (name, car[0])
        if dt == BF16:
            ap = ap.bitcast(BF16)[:, 0:n]
        else:
            ap = ap[:, 0:n]
        if len(shape) == 3:
            ap = ap.rearrange("p (a b) -> p a b", a=shape[1])
        elif len(shape) == 4:
            ap = ap.rearrange("p (a b c) -> p a b c", a=shape[1], b=shape[2])
        return TT(ap, name)

    def phase():
        P.barrier()
        car[0] = 0

    psall = es.enter_context(nc.psum_tensor("psall", [128, 8, 512], F32))
    banks = [TT(psall[:, i, :], f"bank{i}") for i in range(8)]
    bank_ctr = [0]

    def nb():
        b = banks[bank_ctr[0] % 8]
        bank_ctr[0] += 1
        return b

    cst_f = carve("cst_f", [128, 8, 128])
    cst = sb("cst", [128, 8, 128], BF16)
    P.dma("sp", cst_f[:], consts, [], [cst_f.res], "c_cst")
    P.cp("dve", cst[:], cst_f[:], [cst_f.res], [cst.res])
    ident = cst[:, 0, :]
    negtri = cst[:, 1, :]
    negones = cst[:, 2, :]
    m_sb = cst[:, 3, :]
    m_cur = cst[:, 4, :]
    m_prev = cst[:, 5, :]
    bones64 = cst[:, 6, :]
    ones128 = cst[:, 7, :]
    mask4 = sb("mask4", [128, 2, 4, 128], BF16)
    for q in range(4):
        P.cp("dve", mask4[:, 0, q, :], cst_f[:, 5, :], [cst_f.res], [mask4.res])
        P.cp("dve", mask4[:, 1, q, :], cst_f[:, 4, :], [cst_f.res], [mask4.res])
    CR = [cst.res]

    def wcast(dst, src_rows_cols, res, chan):
        P.dma("pool", dst, src_rows_cols.rearrange("(k p) c -> p k c", p=128), [], [res], chan)

    for l in range(L):
        wcast(WSB[l], w_sb[l], RWSB[l], f"c_wsb{l}")
        for c0, n_ in [(0, 512), (512, 256), (768, 256), (1024, 512), (1536, 512)] + [(2048 + 512 * d_, 512) for d_ in range(8)]:
            wcast(WC[l][:, 8 * c0:8 * (c0 + n_)].rearrange("p (k c) -> p k c", k=8), w_c[l][:, c0:c0 + n_], RWC[l], f"c_wc{l}")
        for n in range(4):
            for d in range(8):
                P.dma("pool", WBR[l][:, d, n, :, :],
                      w_br[l, n, :, d * 128:(d + 1) * 128].rearrange("(k p) c -> p k c", p=128),
                      [], [RWBR[l]], f"c_wbr{l}")
        for c0 in (0, 512):
            wcast(WOUT[l][:, 8 * c0:8 * (c0 + 512)].rearrange("p (k c) -> p k c", k=8), w_out[l][:, c0:c0 + 512], RWOUT[l], f"c_wout{l}")
        wcast(WMEM[l], w_mem[l], RWMEM[l], f"c_wmem{l}")

    cv = sb("cv", [128, NCV])
    rv = sb("rv", [128, NRV])
    cvx = sb("cvx", [128, 8])
    cvh = sb("cvh", [128, 12])
    exps = sb("exps", [128, 8])
    lwb = sb("lwb", [128, 2, 4, 128], BF16)
    SS = [sb(f"ss{i}", [128, 4]) for i in range(4)]
    mkT = sb("mkT", [128, 4, 256], BF16)
    mva = sb("mva", [128, 2, 4, 129], BF16)
    P.memset("pool", mva[:], 1.0, [mva.res])
    epsb = sb("epsb", [128, 2])
    P.memset("dve", epsb[:, 0:1], EPS, [epsb.res])
    P.memset("dve", epsb[:, 1:2], 1.0, [epsb.res])
    junkh = [None]

    def rms_rows(xs, gain_tile, out_bf, sst):
        junk = junkh[0]
        P.act(junk[:], xs[:], AF.Square, [xs.res], [junk.res, sst.res], accum=sst[:, 0:1])
        P.act(sst[:, 1:2], sst[:, 0:1], AF.Sqrt, [sst.res], [sst.res], scale=1.0 / 1024, bias=epsb[:, 0:1])
        P.recip(sst[:, 2:3], sst[:, 1:2], [sst.res], [sst.res])
        P.stt("dve", out_bf[:], xs[:], sst[:, 2:3], gain_tile[:], ALU.mult, ALU.mult,
              [xs.res, sst.res, gain_tile.res], [out_bf.res])

    def transpose_to(src_bf, dst_ap_fn, nchunk, evac_eng):
        b = nb()
        bv = b.t[:].bitcast(BF16)
        for c in range(nchunk):
            P.tr(bv[:, c * 128:(c + 1) * 128], src_bf[:, c * 128:(c + 1) * 128], ident, [src_bf.res] + CR, [b.res])
        return b, bv

    for l in range(L):
        phase()
        gbc = carve("gbc", [128, 1024])
        lw = carve("lw", [128, 2, 4, 128])
        wbig = carve("wbig", [128, 8, 1024], BF16)
        XS = [carve(f"xs{i}", [128, 1024]) for i in range(4)]
        junk = carve("junk", [128, 1024], BF16)
        junkh[0] = junk
        ubf = [carve(f"ubf{i}", [128, 1024], BF16) for i in range(2)]
        ubf4 = ubf + [carve(f"ubf{i}", [128, 1024], BF16) for i in range(2, 4)]
        UTB = [carve(f"utb{i}", [128, 8, 512], BF16) for i in range(2)]
        QZT = [carve(f"qzt{i}", [128, 4, 512], BF16) for i in range(2)]
        KTT = [carve(f"ktt{i}", [128, 2, 512], BF16) for i in range(2)]
        VTT = [carve(f"vtt{i}", [128, 4, 256], BF16) for i in range(2)]
        mnT = carve("mnT", [128, 8, 256], BF16)
        mkn = carve("mkn", [128, 512], BF16)
        for q in QZT:
            P.memset("pool", q[:], 0.0, [q.res])
        P.dma("sp", gbc[:], gain_bc[l], [], [gbc.res], "c_gbc")
        P.dma("sp", cv[:], colv[l], [], [cv.res], "c_cv")
        P.dma("sp", rv[:], rowv[l], [], [rv.res], "c_rv")
        P.dma("sp", lw[:], lruw[l].rearrange("a c p j -> p a c j"), [], [lw.res], "c_lw")
        P.cp("dve", lwb[:], lw[:], [lw.res], [lwb.res])
        P.ts("dve", cvx[:, 0:1], cv[:, 0:1], 0.125, None, ALU.mult, ALU.bypass, [cv.res], [cvx.res])
        P.ts("dve", cvx[:, 1:2], cv[:, 2:3], float(128 ** -0.5), None, ALU.mult, ALU.bypass, [cv.res], [cvx.res])
        P.act(cvx[:, 2:6], cv[:, 31:35], AF.Exp, [cv.res], [cvx.res], scale=-1.0)
        P.act(cvx[:, 2:6], cvx[:, 2:6], AF.Ln, [cvx.res], [cvx.res], bias=epsb[:, 1:2])
        P.ts("dve", cvx[:, 2:6], cvx[:, 2:6], -8.0, None, ALU.mult, ALU.bypass, [cvx.res], [cvx.res])
        P.act(exps[:], rv[:, 0:8], AF.Exp, [rv.res], [exps.res])
        P.ts("dve", cvh[:, 0:8], cv[:, 23:31], 0.5, None, ALU.mult, ALU.bypass, [cv.res], [cvh.res])
        P.ts("dve", cvh[:, 8:12], cvx[:, 2:6], 0.5, None, ALU.mult, ALU.bypass, [cvx.res], [cvh.res])

        P.dma("sp", wbig[:, :, 0:1024], WMEM[l], [RWMEM[l]], [wbig.res], "c_wbig")
        mg = XS[3]
        P.dma("sp", mg[:], mgain_bc[l], [], [mg.res], "c_xs3")
        for r in range(2):
            xs = XS[r]
            P.dma("sp", xs[:], mem_in[r * 128:(r + 1) * 128, :], [], [xs.res], f"c_xs{r}")
            rms_rows(xs, mg, ubf[r], SS[r])
            b, bv = transpose_to(ubf[r], None, 8, "act")
            P.cp("act", mnT[:, :, r * 128:(r + 1) * 128], bv.rearrange("p (c t) -> p c t", c=8), [b.res], [mnT.res])
        for r in range(2):
            b = nb()
            for k in range(8):
                P.mm(b[:], mnT[:, k, r * 128:(r + 1) * 128], wbig[:, k, 0:512], k == 0, k == 7, [mnT.res, wbig.res], [b.res])
            sst = SS[2 + r]
            for hm in range(4):
                P.act(junk[:, 0:128], b[:, hm * 128:(hm + 1) * 128], AF.Square, [b.res], [junk.res, sst.res],
                      accum=sst[:, hm:hm + 1])
            P.act(sst[:], sst[:], AF.Sqrt, [sst.res], [sst.res], scale=1.0 / 128, bias=epsb[:, 0:1])
            P.recip(sst[:], sst[:], [sst.res], [sst.res])
            for hm in range(4):
                P.stt("dve", mkn[:, hm * 128:(hm + 1) * 128], b[:, hm * 128:(hm + 1) * 128], sst[:, hm:hm + 1],
                      rv[:, 8:136], ALU.mult, ALU.mult, [b.res, sst.res, rv.res], [mkn.res])
            b2, bv2 = transpose_to(mkn, None, 4, "act")
            P.cp("act", mkT[:, :, r * 128:(r + 1) * 128], bv2[:, 0:512].rearrange("p (c t) -> p c t", c=4), [b2.res], [mkT.res])
            b = nb()
            for k in range(8):
                P.mm(b[:], mnT[:, k, r * 128:(r + 1) * 128], wbig[:, k, 512:1024], k == 0, k == 7, [mnT.res, wbig.res], [b.res])
            P.cp("act", mva[:, r, :, 0:128], b[:].rearrange("p (h d) -> p h d", h=4), [b.res], [mva.res])

        P.dma("sp", wbig[:, :, 0:768], WSB[l], [RWSB[l]], [wbig.res], "c_wbig")

        def gen_norm(i, j):
            uT = UTB[i % 2]
            row0 = i * 512 + j * 128
            xs = XS[j]
            if l == 0:
                P.dma("sp", xs[:], x_in[row0:row0 + 128, :], [], [xs.res], f"c_xs{j}")
            else:
                xr0 = (i % 2) * 512 + j * 128
                P.dma("sp", xs[:], X1F[i // 2][xr0:xr0 + 128, :], [RX1F[i // 2]], [xs.res], f"c_xs{j}")
            yield
            u = ubf4[j]
            rms_rows(xs, gbc, u, SS[j])
            yield
            b, bv = transpose_to(u, None, 8, "act")
            P.cp("act" if j % 2 == 0 else "dve", uT[:, :, j * 128:(j + 1) * 128],
                 bv.rearrange("p (c t) -> p c t", c=8), [b.res], [uT.res])

        def gen_sb(i):
            uT = UTB[i % 2]
            qz, kt, vt = QZT[i % 2], KTT[i % 2], VTT[i % 2]
            for c in range(2):
                yield
                b = nb()
                for k in range(8):
                    P.mm(b[:], wbig[:, k, c * 128:(c + 1) * 128], uT[:, k, :], k == 0, k == 7, [wbig.res, uT.res], [b.res])
                P.act(qz[0:64, 2 * c, :], b[0:64, :], AF.Copy, [b.res], [qz.res], scale=0.125)
                P.ts("dve", qz[64:128, 2 * c + 1, :], b[64:128, :], 0.125, None, ALU.mult, ALU.bypass, [b.res], [qz.res])
            for c in range(2):
                yield
                b = nb()
                for k in range(8):
                    P.mm(b[:], wbig[:, k, 256 + c * 128:256 + (c + 1) * 128], uT[:, k, :], k == 0, k == 7,
                         [wbig.res, uT.res], [b.res])
                P.cp("act" if c % 2 == 0 else "dve", kt[:, c, :], b[:], [b.res], [kt.res])
            for j in range(4):
                yield
                b = nb()
                for k in range(8):
                    P.mm(b[:, 0:256], uT[:, k, j * 128:(j + 1) * 128], wbig[:, k, 512:768], k == 0, k == 7,
                         [wbig.res, uT.res], [b.res])
                P.cp("act" if j % 2 == 0 else "dve", vt[:, j, :], b[:, 0:256], [b.res], [vt.res])
            P.dma("pool", QZ[:, :, i * 512:(i + 1) * 512], qz[:], [qz.res], [RQZ[i]], f"c_qzt{i % 2}")
            P.dma("pool", KT[:, :, i * 512:(i + 1) * 512], kt[:], [kt.res], [RKT[i]], f"c_ktt{i % 2}")
            P.dma("pool", VS[4 * i:4 * i + 4].rearrange("j p c -> p j c"), vt[:], [vt.res], [RVS[i]], f"c_vtt{i % 2}")

        def run_il(gens):
            gens = list(gens)
            while gens:
                for gq in list(gens):
                    try:
                        next(gq)
                    except StopIteration:
                        gens.remove(gq)

        for i in range(NT):
            gl = [gen_norm(i, j) for j in range(4)]
            if i > 0:
                gl.append(gen_sb(i - 1))
            run_il(gl)
            P.dma("pool", UT[:, :, i * 512:(i + 1) * 512], UTB[i % 2][:], [UTB[i % 2].res], [RUT[i]], f"c_utb{i % 2}")
        run_il([gen_sb(NT - 1)])

        phase()
        ybh_views = [t.rearrange("(b p) c -> b p c", p=128) for t in YBH[l]]
        pass_B(P, carve, l, S, NT, NB, banks, psall, QZ, KT, VS, ybh_views, NBP, RQZ, RKT, RVS, RYB, cst, CR)

        phase()
        pass_C(P, carve, l, L, S, NT, locals())

    fin = [(c, n) for c, n in P.count.items() if c.startswith("c_")]
    P.streams["pool"].append((fin, None, None, 0))

    sems = {c: es.enter_context(nc.semaphore("s_" + c)) for c in P.chans}
    block = es.enter_context(nc.Block())

    def emit(stream):
        def f(eng):
            for waits, fn, ch, inc in stream:
                for c, n in waits:
                    eng.wait_ge(sems[c], n)
                if fn is not None:
                    r = fn(eng)
                    if ch is not None:
                        r.then_inc(sems[ch], inc)
        return f

    block.tensor(emit(P.streams["pe"]))
    block.scalar(emit(P.streams["act"]))
    block.vector(emit(P.streams["dve"]))
    block.gpsimd(emit(P.streams["pool"]))
    block.sync(emit(P.streams["sp"]))
    es.close()
    return nc


def pass_B(P, carve, l, S, NT, NB, banks, psall, QZ, KT, VS, YBV, NBP, RQZ, RKT, RVS, RYB, cst, CR):
    KB = 32
    NS = KB
    NR = KB + 8
    qzp = carve("b_qzp", [128, 2, S], BF16)
    ktp = carve("b_ktp", [128, S], BF16)
    vp = carve("b_vp", [128, NB, 128], BF16)
    LBt = carve("b_lall", [128, NS, 512], BF16)
    LB = [TT(LBt[:, i, :], f"b_l{i}") for i in range(NS)]
    RBT = [carve(f"b_r{i}", [128, 512], BF16) for i in range(NR)]
    ABt = carve("b_aall", [128, NS, 512], BF16)
    AB = [TT(ABt[:, i, :], f"b_a{i}") for i in range(NS)]
    YT = [carve(f"b_y{i}", [128, 4, 128], BF16) for i in range(2)]
    ident, negtri, negones, m_sb = cst[:, 0, :], cst[:, 1, :], cst[:, 2, :], cst[:, 3, :]
    ZP = [(0, 1), (2, 3)]
    ACC = [[banks[4], banks[5]], [banks[6], banks[7]]]
    NCH = min(4, NT)
    CT = S // NCH
    qres = [Res(f"b_qres{c}") for c in range(NCH)]
    kres = [Res(f"b_kres{c}") for c in range(NCH)]
    vres = [Res(f"b_vres{c}") for c in range(NCH)]
    for hp in range(2):
        for c in range(NCH):
            t0, t1 = c * CT, (c + 1) * CT
            P.dma("sp", qzp[:, :, t0:t1], QZ[:, 2 * hp:2 * hp + 2, t0:t1], RQZ, [qres[c]], f"c_qzp{c}")
            P.dma("sp", ktp[:, t0:t1], KT[:, hp, t0:t1], RKT, [kres[c]], f"c_ktp{c}")
            b0, b1 = t0 // 128, t1 // 128
            P.dma("sp", vp[:, b0:b1, :], VS[b0:b1, :, hp * 128:(hp + 1) * 128].rearrange("b p c -> p b c"),
                  RVS, [vres[c]], f"c_vp{c}")
        items = []
        for i in range(NT):
            nblk = 4 * i + 4
            for n, kb in enumerate(reversed(range(nblk))):
                for s in range(2):
                    jd = kb - 4 * i if kb >= 4 * i else -1
                    items.append(dict(s=s, i=i, kb=kb, n=n, jd=jd, c0=(128 * jd if jd >= 0 else 0),
                                      first=(n == 0), last=(n == nblk - 1)))
        NI = len(items)

        def zmm(w, zb):
            it = items[w]
            c0, s, i, kb = it["c0"], it["s"], it["i"], it["kb"]
            diag = it["jd"] >= 0
            P.mm(zb[:, c0:512], ktp[:, kb * 128:(kb + 1) * 128], qzp[:, s, i * 512 + c0:(i + 1) * 512],
                 True, False, [kres[(kb * 128) // CT], qres[(i * 512) // CT]], [zb.res])
            if diag:
                P.mm(zb[:, c0:c0 + 128], ident, m_sb, False, False, CR, [zb.res])

        def rupd(w):
            it = items[w]
            lt = LB[w % NS]
            c0 = it["c0"]
            if not it["last"]:
                rn = RBT[(w + 2) % NR]
                if it["first"]:
                    if c0 > 0:
                        P.memset("pool", rn[:, 0:c0], 0.0, [rn.res])
                    P.cp("pool", rn[:, c0:512], lt[:, c0:512], [lt.res], [rn.res])
                else:
                    ro = RBT[w % NR]
                    if c0 > 0:
                        P.cp("pool", rn[:, 0:c0], ro[:, 0:c0], [ro.res], [rn.res])
                    P.tt("dve", rn[:, c0:512], ro[:, c0:512], lt[:, c0:512], ALU.add, [ro.res, lt.res], [rn.res])

        def P1(w):
            it = items[w]
            c0 = it["c0"]
            bp = ZP[(w // 2) % 2]
            zmm(w, banks[bp[0]])
            zmm(w + 1, banks[bp[1]])
            s0 = w % NS
            P.act(LBt[:, s0:s0 + 2, c0:512], psall[:, bp[0]:bp[0] + 2, c0:512], AF.Softplus,
                  [banks[bp[0]].res, banks[bp[1]].res], [LB[s0].res, LB[s0 + 1].res])
            rupd(w)
            rupd(w + 1)

        def P2a(w):
            bp = ZP[(w // 2) % 2]
            for q in range(2):
                it = items[w + q]
                zb, lt = banks[bp[q]], LB[(w + q) % NS]
                c0 = it["c0"]
                zmm(w + q, zb)
                P.mm(zb[:, c0:512], negtri, lt[:, c0:512], False, it["first"], [lt.res] + CR, [zb.res])
                if not it["first"]:
                    ro = RBT[(w + q) % NR]
                    P.mm(zb[:, c0:512], negones, ro[:, c0:512], False, True, [ro.res] + CR, [zb.res])

        def P2b(w):
            it = items[w]
            c0 = it["c0"]
            bp = ZP[(w // 2) % 2]
            s0 = w % NS
            P.act(ABt[:, s0:s0 + 2, c0:512], psall[:, bp[0]:bp[0] + 2, c0:512], AF.Exp,
                  [banks[bp[0]].res, banks[bp[1]].res], [AB[s0].res, AB[s0 + 1].res])

        def P2c(w):
            it = items[w]
            a = AB[w % NS]
            s, i, kb = it["s"], it["i"], it["kb"]
            acc = ACC[s][i % 2]
            j0 = max(it["jd"], 0)
            for j in range(j0, 4):
                P.mm(acc[:, j * 64:(j + 1) * 64], a[:, j * 128:(j + 1) * 128], vp[:, kb, s * 64:(s + 1) * 64],
                     it["first"] and j == j0, it["last"] and j == 3, [a.res, vres[(kb * 128) // CT]], [acc.res])
            if it["last"]:
                yt = YT[i % 2]
                P.cp("dve", yt[:, :, s * 64:(s + 1) * 64], acc[:, 0:256].rearrange("p (j d) -> p j d", j=4),
                     [acc.res], [yt.res])
                if s == 1:
                    ybq, ybo = YBV[(4 * i) // NBP], (4 * i) % NBP
                    P.dma("pool", ybq[ybo:ybo + 4, :, hp * 128:(hp + 1) * 128].rearrange("j p c -> p j c"), yt[:],
                          [yt.res], [RYB[i]], f"c_byt{i % 2}")

        nxt = 0
        for b0 in range(0, NI, KB):
            b1 = min(NI, b0 + KB)
            for w in range(b0, b1, 2):
                P1(w)
                for _ in range(2):
                    if nxt < b0:
                        P2c(nxt)
                        nxt += 1
            while nxt < b0:
                P2c(nxt)
                nxt += 1
            P2a(b0)
            for w in range(b0, b1, 2):
                if w + 2 < b1:
                    P2a(w + 2)
                P2b(w)
        while nxt < NI:
            P2c(nxt)
            nxt += 1


def pass_C(P, carve, l, L, S, NT, env):
    g = env
    banks = g["banks"]
    nbc = [0]

    def nb():
        b = banks[nbc[0] % 6]
        nbc[0] += 1
        return b
    cst, CR = g["cst"], g["CR"]
    ident, bones64, ones128 = cst[:, 0, :], cst[:, 6, :], cst[:, 7, :]
    mask4 = g["mask4"]
    cv, rv, cvx, exps, lwb, epsb, cvh = g["cv"], g["rv"], g["cvx"], g["exps"], g["lwb"], g["epsb"], g["cvh"]
    mkT, mva = g["mkT"], g["mva"]
    UT, WC, WBR, WOUT = g["UT"], g["WC"], g["WBR"], g["WOUT"]
    RUT, RYB, RWC, RWBR, RWOUT = g["RUT"], g["RYB"], g["RWC"], g["RWBR"], g["RWOUT"]
    NTH, rkv, YTSH, YTSF, RYTS, RYTSF = g["NTH"], g["rkv"], g["YTSH"], g["YTSF"], g["RYTS"], g["RYTSF"]
    YBHv, NBP = g["ybh_views"], g["NBP"]
    X1F, RX1F, x_own, y_out, RG = g["X1F"], g["RX1F"], g["x_own"], g["y_out"], g["RG"]
    X1H, RX1H = g["X1HL"][(l + 1) % 2], g["RX1HL"][(l + 1) % 2]
    XN, RXN = g["X1HL"][l % 2], g["RX1HL"][l % 2]
    _PC = {"ctr": 0, "mctr": 0, "oactr": 0}
    uTd = [carve(f"c_ut{i}", [128, 8, 512], BF16) for i in range(2)]
    WY = carve("c_wy", [128, 8, 2048], BF16)
    uTx = carve("c_utx", [128, 8, 512], BF16)
    yTx = carve("c_ytx", [128, 4, 4, 512], BF16)
    XSc = [carve(f"c_xs{i}", [128, 512]) for i in range(2)]
    WCS = [None, None, None] + [carve(f"c_wcs{i}", [128, 8, 512], BF16) for i in range(3, 5)]
    WBS = [carve(f"c_wbs{i}", [128, 4, 4, 128], BF16) for i in range(2)]
    qa = [carve(f"c_qa{h}", [128, 2, 512], BF16) for h in range(2)]
    kn = [[carve(f"c_kn{a}{i}", [128, 512], BF16) for i in range(2)] for a in range(1)]
    va = [carve(f"c_va{i}", [128, 4, 2, 65], BF16) for i in range(2)]
    sq = [carve(f"c_sq{i}", [128, 512], BF16) for i in range(2)]
    rs = [carve(f"c_rs{i}", [128, 512]) for i in range(2)]
    sg = [carve(f"c_sg{n}", [128, 4, 256], BF16) for n in range(3)]
    ysb = sg
    ybt = carve("c_ybt", [128, 4, 256], BF16)
    yTall = carve("c_yTall", [128, 8, 512], BF16)

    class _V:
        def __init__(self, t, n):
            self.t, self.n, self.res = t, n, t.res

        def __getitem__(self, k):
            if isinstance(k[1], slice):
                s0 = 2 * self.n + (k[1].start or 0)
                s1 = 2 * self.n + (k[1].stop if k[1].stop is not None else 2)
                return self.t[(k[0], slice(s0, s1)) + tuple(k[2:])]
            return self.t[(k[0], 2 * self.n + k[1]) + tuple(k[2:])]
    yT = [_V(yTall, n) for n in range(4)]
    psb_s = [carve(f"c_ps{i}", [128, 512], BF16) for i in range(2)]
    psb_m = [carve(f"c_pm{i}", [128, 512], BF16) for i in range(3)]
    den = [carve(f"c_den{i}", [128, 8]) for i in range(4)]
    qmn = carve("c_qmn", [128, 2, 512], BF16)
    cx = carve("c_cx", [128, 2, 515])
    ht = [carve("c_ht0", [128, 512])] * 2
    hcar = carve("c_hcar", [128, 8])
    xc = [carve("c_xc0", [128, 512])] * 2
    xcb = [carve("c_xcb0", [128, 512], BF16)] * 2
    lt = [carve(f"c_lt{q}", [128, 512]) for q in range(4)]
    sgc = carve("c_sgc", [128, 2, 512], BF16)
    gs = [carve(f"c_gs{i}", [128, 512]) for i in range(2)]
    tq = [carve(f"c_tq{i}", [128, 512]) for i in range(2)]
    tqc = [0]

    def silu2(out_ap, out_res, b_ap, b_res=None):
        if b_res is None:
            b_ap, b_res = b_ap[:], b_ap.res
        n = b_ap.shape[-1]
        t = tq[tqc[0] % 2]
        tqc[0] += 1
        P.act(t[:, 0:n], b_ap, AF.Tanh, [b_res], [t.res], scale=0.5)
        P.stt("dve", out_ap, t[:, 0:n], 1.0, b_ap, ALU.add, ALU.mult, [t.res, b_res], [out_res])
    mx = [carve(f"c_mx{i}", [128, 512]) for i in range(3)]
    mixT = carve("c_mixT", [128, 8, 512], BF16)
    xo = [carve(f"c_xo{i}", [128, 512]) for i in range(2)]
    for q in qa:
        P.memset("pool", q[:], 0.0, [q.res])
    for q in va:
        P.memset("pool", q[:], 1.0, [q.res])
    P.memset("pool", cx[:], 0.0, [cx.res])
    P.memset("pool", hcar[:], 0.0, [hcar.res])

    class _W:
        def __init__(self, t, off):
            self.t, self.off, self.res = t, off, t.res

        def __getitem__(self, k):
            s = k[2]
            return self.t[(k[0], k[1], slice(self.off + s.start, self.off + s.stop))]

    for c0_, n_ in [(0, 512), (512, 256), (768, 256), (1024, 512), (1536, 512)]:
        P.dma("sp", WY[:, :, c0_:c0_ + n_], WC[l][:, 8 * c0_:8 * (c0_ + n_)].rearrange("p (k c) -> p k c", k=8),
              [RWC[l]], [WY.res], "c_wy")

    def load_wc(col0, ncols, slot, src=None, res=None):
        if slot < 3:
            return _W(WY, col0)
        w = WCS[slot]
        if src is None:
            src, res = WC[l], RWC[l]
        P.dma("sp", w[:, :, 0:ncols], src[:, 8 * col0:8 * (col0 + ncols)].rearrange("p (k c) -> p k c", k=8),
              [res], [w.res], "c_" + w.name)
        return w

    def proj_fm(w, col0, uT):
        b = nb()
        for k in range(8):
            P.mm(b[:], w[:, k, col0:col0 + 128], uT[:, k, :], k == 0, k == 7, [w.res, uT.res], [b.res])
        return b

    def proj_tm(w, col0, ncol, uT, j):
        b = nb()
        for k in range(8):
            P.mm(b[:, 0:ncol], uT[:, k, j * 128:(j + 1) * 128], w[:, k, col0:col0 + ncol], k == 0, k == 7,
                 [w.res, uT.res], [b.res])
        return b

    tog = [0]

    def qknorm(b, onesmat, nfeat, writes):
        q = tog[0] % 2
        tog[0] += 1
        P.act(sq[q][:], b[:], AF.Square, [b.res], [sq[q].res])
        b2 = nb()
        P.mm(b2[:], onesmat, sq[q][:], True, True, [sq[q].res] + CR, [b2.res])
        P.act(rs[q][:], b2[:], AF.Sqrt, [b2.res], [rs[q].res], bias=epsb[:, 0:1])
        P.recip(rs[q][:], rs[q][:], [rs[q].res], [rs[q].res])
        return rs[q]

    UTv = UT.rearrange("p c (t h s) -> h t p c s", h=2, s=512)

    def stage_X(t):
        P.op("sp", lambda e: e.dma_start(out=uTx[:], in_=UTv[rkv["r"], t]),
             RUT, [uTx.res], chan="c_utx")
        ysf = YTSF[t].rearrange("(r h p n c) s -> r h p n c s", r=2, h=2, p=128, n=4)
        for sr in range(2):
            P.op("sp", lambda e, sr=sr: e.dma_start(out=yTx[:, :, 2 * sr:2 * sr + 2, :], in_=ysf[sr, rkv["r"]]),
                 [RYTSF[t]], [yTx.res], chan="c_ytx")
        xsrcs = [(2048 + d_ * 512, None, None) for d_ in range(8)] + [(ch_ * 512, WOUT[l], RWOUT[l]) for ch_ in range(2)]

        def xload(q):
            c0, s_, r_ = xsrcs[q]
            return load_wc(c0, 512, 3 + q % 2, s_, r_)
        wnext = xload(0)
        for d in range(8):
            yield
            w = wnext
            wnext = xload(d + 1)
            wb = WBS[d % 2]
            P.dma("sp", wb[:], WBR[l][:, d], [RWBR[l]], [wb.res], "c_" + wb.name)
            for n in range(4):
                yield
                bg = proj_fm(w, n * 128, uTx)
                gn = gs[n % 2]
                P.act(gn[:], bg[:], AF.Tanh, [bg.res], [gn.res], scale=0.5)
                bu = nb()
                for kc in range(4):
                    P.mm(bu[:], wb[:, n, kc, :], yTx[:, n, kc, :], kc == 0, kc == 3, [wb.res, yTx.res], [bu.res])
                if n == 0:
                    P.stt("dve", mx[0][:], gn[:], 1.0, bu[:], ALU.add, ALU.mult, [gn.res, bu.res], [mx[0].res])
                else:
                    tmp = mx[1 + n % 2]
                    P.stt("dve", tmp[:], gn[:], 1.0, bu[:], ALU.add, ALU.mult, [gn.res, bu.res], [tmp.res])
                    if n < 3:
                        P.tt("dve", mx[0][:], mx[0][:], tmp[:], ALU.add, [mx[0].res, tmp.res], [mx[0].res])
                    else:
                        P.tt("dve", mixT[:, d, :], mx[0][:], tmp[:], ALU.add, [mx[0].res, tmp.res], [mixT.res])
        for ch in range(2):
            yield
            w = wnext
            if ch == 0:
                wnext = xload(9)
            for j in range(4):
                row0 = t * 512 + j * 128
                xs = XSc[j % 2]
                if l == 0:
                    P.dma("sp", xs[:], x_own[row0:row0 + 128, ch * 512:(ch + 1) * 512], [], [xs.res], "c_" + xs.name)
                else:
                    P.dma("sp", xs[:], X1H[t][j * 128:(j + 1) * 128, ch * 512:(ch + 1) * 512], [RX1H[t]], [xs.res], "c_" + xs.name)
                yield
                b = nb()
                for k in range(8):
                    P.mm(b[:], mixT[:, k, j * 128:(j + 1) * 128], w[:, k, :], k == 0, k == 7, [mixT.res, w.res], [b.res])
                o = xo[j % 2]
                P.stt("dve", o[:], b[:], 0.5, xs[:], ALU.mult, ALU.add, [b.res, xs.res], [o.res])
                if l == L - 1:
                    P.dma("pool", y_out[row0:row0 + 128, ch * 512:(ch + 1) * 512], o[:], [o.res], [], "c_" + o.name)
                else:
                    P.dma("pool", XN[t][j * 128:(j + 1) * 128, ch * 512:(ch + 1) * 512], o[:], [o.res], [RXN[t]], "c_" + o.name)
        if l < L - 1:
            P.op("pool", lambda e: e.collective_compute("AllGather", ALU.bypass, RG, [XN[t].opt()], [X1F[t].opt()]),
                 [RXN[t]], [RX1F[t]], chan="c_cc", inc=1)

    def gen_swa(i):
        pi = i % 2
        uT = uTd[i % 2]
        yield
        w = load_wc(0, 512, 0)
        for c in range(2):
            yield
            b = proj_fm(w, c * 128, uT)
            r = qknorm(b, bones64, 64, None)
            P.stt("dve", qa[0][0:64, c, :], b[0:64, :], cvx[0:64, 0:1], r[0:64, :], ALU.mult, ALU.mult,
                  [b.res, r.res, cvx.res], [qa[0].res])
            P.stt("dve", qa[1][64:128, c, :], b[64:128, :], cvx[64:128, 0:1], r[64:128, :], ALU.mult, ALU.mult,
                  [b.res, r.res, cvx.res], [qa[1].res])
        for a in range(1):
            yield
            b = proj_fm(w, 256, uT)
            r = qknorm(b, bones64, 64, None)
            P.stt("dve", kn[a][pi][:], b[:], cv[:, 1:2], r[:], ALU.mult, ALU.mult, [b.res, r.res, cv.res], [kn[a][pi].res])
        for j in range(4):
            yield
            b = proj_tm(w, 384, 64, uT, j)
            P.cp("act", va[pi][:, j, 0, 0:64], b[:, 0:64], [b.res], [va[pi].res])
        yield
        w = load_wc(512, 256, 0)
        for j in range(4):
            yield
            b = proj_tm(w, 0, 256, uT, j)
            silu2(sg[0][:, j, 0:256], sg[0].res, b[:, 0:256], b.res)
        for j in range(4):
            nblk = 4 * i + j
            for gq in range(1):
                yield
                oa = banks[6 + _PC["oactr"] % 2]
                _PC["oactr"] += 1
                kbs = ([0] if nblk > 0 else []) + [1]
                for kk, which in enumerate(kbs):
                    if which == 1:
                        ksrc = lambda a: kn[a][pi][:, j * 128:(j + 1) * 128]
                        kres = [kn[0][pi].res]
                        vsrc, vres = va[pi][:, j, gq, :], va[pi].res
                    else:
                        if j > 0:
                            ksrc = lambda a: kn[a][pi][:, (j - 1) * 128:j * 128]
                            kres = [kn[0][pi].res]
                            vsrc, vres = va[pi][:, j - 1, gq, :], va[pi].res
                        else:
                            ksrc = lambda a: kn[a][1 - pi][:, 384:512]
                            kres = [kn[0][1 - pi].res]
                            vsrc, vres = va[1 - pi][:, 3, gq, :], va[1 - pi].res
                    yield
                    sc = nb()
                    for hq in range(4):
                        h = 4 * gq + hq
                        a = 0
                        P.mm(sc[:, hq * 128:(hq + 1) * 128], ksrc(a), qa[h % 2][:, h // 2, j * 128:(j + 1) * 128],
                             hq == 0, False, kres + [qa[0].res, qa[1].res], [sc.res])
                    P.mm(sc[:], ident, mask4[:, which, :, :].rearrange("p a b -> p (a b)"), False, True,
                         [mask4.res] + CR, [sc.res])
                    pt = psb_s[_PC["ctr"] % 2]
                    _PC["ctr"] += 1
                    P.act(pt[:], sc[:], AF.Exp, [sc.res], [pt.res])
                    for hq in range(4):
                        P.mm(oa[:, hq * 65:(hq + 1) * 65], pt[:, hq * 128:(hq + 1) * 128], vsrc,
                             kk == 0 and hq == 0, kk == len(kbs) - 1 and hq == 3, [pt.res, vres], [oa.res])
                dn = den[(2 * j + gq) % 4]
                oav = oa[:, 0:260].rearrange("p (h d) -> p h d", h=4)
                P.tt("dve", dn[:, 0:4], oav[:, :, 64], exps[:, 4 * gq:4 * gq + 4], ALU.add, [oa.res, exps.res], [dn.res])
                P.ts("dve", dn[:, 0:4], dn[:, 0:4], 2.0, None, ALU.mult, ALU.bypass, [dn.res], [dn.res])
                P.recip(dn[:, 4:8], dn[:, 0:4], [dn.res], [dn.res])
                for hq in range(4):
                    h = 4 * gq + hq
                    P.stt("dve", ysb[0][:, j, h * 64:(h + 1) * 64], oa[:, hq * 65:hq * 65 + 64], dn[:, 4 + hq:5 + hq],
                          sg[0][:, j, h * 64:(h + 1) * 64], ALU.mult, ALU.mult, [oa.res, dn.res, sg[0].res], [ysb[0].res])
        for n3, n in ((0, 0),):
            for j in range(4):
                yield
                b = nb()
                bv = b.t[:].bitcast(BF16)
                for c in range(2):
                    P.tr(bv[:, c * 128:(c + 1) * 128], ysb[n3][:, j, c * 128:(c + 1) * 128], ident, [ysb[n3].res] + CR, [b.res])
                P.cp("act" if j % 2 == 0 else "dve", yT[n][:, :, j * 128:(j + 1) * 128],
                     bv[:, 0:256].rearrange("p (c t) -> p c t", c=2), [b.res], [yT[n].res])

    def gen_mem(i):
        pi = i % 2
        uT = uTd[i % 2]
        yield
        w = load_wc(768, 256, 1)
        ybq, ybo = YBHv[(4 * i) // NBP], (4 * i) % NBP
        P.dma("sp", ybt[:], ybq[ybo:ybo + 4].rearrange("j p c -> p j c"), [RYB[i]], [ybt.res], "c_ybt")
        for j in range(4):
            yield
            b = proj_tm(w, 0, 256, uT, j)
            silu2(sg[1][:, j, 0:256], sg[1].res, b[:, 0:256], b.res)
            P.stt("dve", ysb[1][:, j, 0:256], ybt[:, j, :], 0.5, sg[1][:, j, 0:256], ALU.mult, ALU.mult, [ybt.res, sg[1].res], [ysb[1].res])
        yield
        w = load_wc(1024, 512, 1)
        for hm in range(2):
            yield
            b = proj_fm(w, hm * 128, uT)
            r = qknorm(b, ones128, 128, None)
            P.stt("dve", qmn[:, hm, :], b[:], cvx[:, 1:2], r[:], ALU.mult, ALU.mult, [b.res, r.res, cvx.res], [qmn.res])
        for j in range(4):
            yield
            b = proj_tm(w, 256, 256, uT, j)
            silu2(sg[2][:, j, 0:256], sg[2].res, b[:, 0:256], b.res)
        for hm in range(2):
            pts = []
            for mb in range(2):
                yield
                sc = nb()
                P.mm(sc[:], mkT[:, hm, mb * 128:(mb + 1) * 128], qmn[:, hm, :], True, True, [mkT.res, qmn.res], [sc.res])
                pt = psb_m[_PC["mctr"] % 3]
                _PC["mctr"] += 1
                P.act(pt[:], sc[:], AF.Exp, [sc.res], [pt.res])
                pts.append(pt)
            for jp in range(2):
                yield
                om = nb()
                for jj in range(2):
                    j = 2 * jp + jj
                    for mb in range(2):
                        P.mm(om[:, jj * 129:(jj + 1) * 129], pts[mb][:, j * 128:(j + 1) * 128], mva[:, mb, hm, :],
                             jj == 0 and mb == 0, jj == 1 and mb == 1, [pts[mb].res, mva.res], [om.res])
                dn = den[(2 * hm + jp) % 4]
                for jj in range(2):
                    j = 2 * jp + jj
                    P.ts("dve", dn[:, 4 + jj:5 + jj], om[:, jj * 129 + 128:jj * 129 + 129], 2.0, None, ALU.mult, ALU.bypass, [om.res], [dn.res])
                    P.recip(dn[:, jj:jj + 1], dn[:, 4 + jj:5 + jj], [dn.res], [dn.res])
                    P.stt("dve", ysb[2][:, j, hm * 128:(hm + 1) * 128], om[:, jj * 129:jj * 129 + 128], dn[:, jj:jj + 1],
                          sg[2][:, j, hm * 128:(hm + 1) * 128], ALU.mult, ALU.mult, [om.res, dn.res, sg[2].res], [ysb[2].res])
        for n3, n in ((1, 1), (2, 3)):
            for j in range(4):
                yield
                b = nb()
                bv = b.t[:].bitcast(BF16)
                for c in range(2):
                    P.tr(bv[:, c * 128:(c + 1) * 128], ysb[n3][:, j, c * 128:(c + 1) * 128], ident, [ysb[n3].res] + CR, [b.res])
                P.cp("act" if j % 2 == 0 else "dve", yT[n][:, :, j * 128:(j + 1) * 128],
                     bv[:, 0:256].rearrange("p (c t) -> p c t", c=2), [b.res], [yT[n].res])

    def gen_lru(i):
        pi = i % 2
        uT = uTd[i % 2]
        yield
        w = load_wc(1536, 512, 2)
        for c in range(2):
            yield
            b = proj_fm(w, c * 128, uT)
            if i > 0:
                P.cp("dve", cx[:, c, 0:3], cx[:, c, 512:515], [cx.res], [cx.res])
            P.cp("act", cx[:, c, 3:515], b[:], [b.res], [cx.res])
        for c in range(2):
            yield
            b = proj_fm(w, 256 + c * 128, uT)
            silu2(sgc[:, c, :], sgc.res, b)
        for c in range(2):
            q = c % 2
            xcc, xcbb = xc[q], xcb[q]
            tr_, ti_, tm_, tb_ = lt[0], lt[1], lt[2], lt[3]
            P.ts("dve", xcc[:], cx[:, c, 0:512], cv[:, 3 + c:4 + c], cv[:, 19 + c:20 + c], ALU.mult, ALU.add,
                 [cx.res, cv.res], [xcc.res])
            for k in range(1, 4):
                P.stt("dve", xcc[:], cx[:, c, k:k + 512], cv[:, 3 + 4 * k + c:4 + 4 * k + c], xcc[:], ALU.mult, ALU.add,
                      [cx.res, cv.res, xcc.res], [xcc.res])
            P.cp("act", xcbb[:], xcc[:], [xcc.res], [xcbb.res])
            yield
            ba = nb()
            P.mm(ba[:], lwb[:, 0, c, :], xcbb[:], True, True, [lwb.res, xcbb.res], [ba.res])
            bx = nb()
            P.mm(bx[:], lwb[:, 1, c, :], xcbb[:], True, True, [lwb.res, xcbb.res], [bx.res])
            P.act(tr_[:], ba[:], AF.Tanh, [ba.res, cvh.res], [tr_.res], scale=0.5, bias=cvh[:, c:c + 1])
            P.act(ti_[:], bx[:], AF.Tanh, [bx.res, cvh.res], [ti_.res], scale=0.5, bias=cvh[:, 4 + c:5 + c])
            P.ts("dve", tm_[:], tr_[:], cvh[:, 8 + c:9 + c], cvh[:, 8 + c:9 + c], ALU.mult, ALU.add, [tr_.res, cvh.res], [tm_.res])
            P.act(tr_[:], tm_[:], AF.Exp, [tm_.res], [tr_.res])
            P.act(tm_[:], tm_[:], AF.Exp, [tm_.res], [tm_.res], scale=2.0)
            P.act(tm_[:], tm_[:], AF.Sqrt, [tm_.res, epsb.res], [tm_.res], scale=-1.0, bias=epsb[:, 1:2])
            P.stt("dve", tb_[:], ti_[:], 1.0, xcc[:], ALU.add, ALU.mult, [ti_.res, xcc.res], [tb_.res])
            P.stt("dve", tb_[:], tb_[:], 0.5, tm_[:], ALU.mult, ALU.mult, [tb_.res, tm_.res], [tb_.res])
            hcur = ht[q]

            def scan(e, o=hcur[:], a=tr_[:], bb=tb_[:], init=hcar[:, c:c + 1]):
                return e.tensor_tensor_scan(out=o, data0=a, data1=bb, initial=init, op0=ALU.mult, op1=ALU.add)
            P.op("dve", scan, [tr_.res, tb_.res, hcar.res], [hcur.res])
            P.cp("dve", hcar[:, c:c + 1], hcur[:, 511:512], [hcur.res], [hcar.res])
            P.stt("dve", yT[2][:, c, :], hcur[:], 0.5, sgc[:, c, :], ALU.mult, ALU.mult, [hcur.res, sgc.res], [yT[2].res])
        yield

    def run_interleaved(gens):
        gens = list(gens)
        while gens:
            for gq in list(gens):
                try:
                    next(gq)
                except StopIteration:
                    gens.remove(gq)

    XSTEPS = 8 * 5 + 2 * 5 + 2
    xdone = 0
    gx = None
    for i in range(NT):
        P.dma("sp", uTd[i % 2][:], UT[:, :, i * 512:(i + 1) * 512], [RUT[i]], [uTd[i % 2].res], f"c_cut{i % 2}")
        gl = [gen_swa(i), gen_mem(i), gen_lru(i)]
        if gx is None and i % 2 == 1 and i >= 3 and xdone == (i - 3) // 2:
            gx = stage_X(xdone)
            xdone += 1
            xleft = XSTEPS
        first_half = (i % 2 == 1)
        budget = (XSTEPS // 2) if first_half else 10 ** 9
        while gl:
            for gq in list(gl):
                try:
                    next(gq)
                except StopIteration:
                    gl.remove(gq)
            if gx is not None and budget > 0:
                try:
                    next(gx)
                    budget -= 1
                except StopIteration:
                    gx = None
        if gx is not None:
            while budget > 0:
                try:
                    next(gx)
                    budget -= 1
                except StopIteration:
                    gx = None
                    break
        ysh = YTSH[i // 2].rearrange("(h p c) s -> h p c s", h=2, p=128)
        P.dma("pool", ysh[i % 2], yTall[:], [yTall.res], [RYTS[i]], "c_ytall")
        if i % 2 == 1:
            p_ = i // 2
            P.op("pool", lambda e, p_=p_: e.collective_compute("AllGather", ALU.bypass, RG, [YTSH[p_].opt()], [YTSF[p_].opt()]),
                 [RYTS[2 * p_], RYTS[2 * p_ + 1]], [RYTSF[p_]], chan="c_cc", inc=1)
    if gx is not None:
        for _ in gx:
            pass
    for t_ in range(xdone, NTH):
        run_interleaved([stage_X(t_)])


def _consts():
    s = np.arange(128)[:, None]
    t = np.arange(128)[None, :]
    c = np.zeros((128, 8, 128), np.float32)
    c[:, 0] = np.eye(128)
    c[:, 1] = -((s >= t).astype(np.float32))
    c[:, 2] = -1.0
    c[:, 3] = np.where(s < t, 0.0, NEG)
    c[:, 4] = np.where(s <= t, 0.0, NEG)
    c[:, 5] = np.where(s > t, 0.0, NEG)
    bo = np.zeros((128, 128), np.float32)
    bo[:64, :64] = 1.0 / 64
    bo[64:, 64:] = 1.0 / 64
    c[:, 6] = bo
    c[:, 7] = 1.0 / 128
    return c


def _prep(inputs, L):
    f = lambda a: np.ascontiguousarray(np.asarray(a, dtype=np.float32))
    shared, per = {}, [{}, {}]
    wi = f(inputs["w_in"])
    wm = f(inputs["w_mem_kv"])
    shared["w_branch"] = f(inputs["w_branch"])
    shared["w_out"] = f(inputs["w_out"])
    shared["gain_bc"] = f(np.broadcast_to(np.asarray(inputs["norm_gain"])[:, None, :], (L, 128, 1024)))
    shared["mgain_bc"] = f(np.broadcast_to(np.asarray(inputs["mem_norm_gain"])[:, None, :], (L, 128, 1024)))
    shared["consts"] = _consts()
    merge = np.concatenate([wi[:, :, O_MERGE + n * 1024 + dd * 128:O_MERGE + n * 1024 + (dd + 1) * 128]
                            for dd in range(8) for n in range(4)], axis=2)
    cw = np.asarray(inputs["conv_w"])
    for r in range(2):
        p = per[r]
        o = 256 * r
        p["w_sb"] = np.ascontiguousarray(np.concatenate([wi[:, :, O_BQ + o:O_BQ + o + 256], wi[:, :, O_BK + o:O_BK + o + 256],
                                                         wi[:, :, O_BV + o:O_BV + o + 256]], axis=2))
        kg = wi[:, :, O_AK + 64 * r:O_AK + 64 * (r + 1)]
        vg = wi[:, :, O_AV + 64 * r:O_AV + 64 * (r + 1)]
        p["w_c"] = np.ascontiguousarray(np.concatenate([
            wi[:, :, O_AQ + o:O_AQ + o + 256], kg, kg, vg, vg,
            wi[:, :, O_AG + o:O_AG + o + 256], wi[:, :, O_BG + o:O_BG + o + 256],
            wi[:, :, O_MQ + o:O_MQ + o + 256], wi[:, :, O_MG + o:O_MG + o + 256],
            wi[:, :, O_CX + o:O_CX + o + 256], wi[:, :, O_CG + o:O_CG + o + 256], merge], axis=2))
        oo = 256 * (1 - r)
        p["w_mem_kv"] = np.ascontiguousarray(np.concatenate([wm[:, :, o:o + 256], wm[:, :, oo:oo + 256],
                                                             wm[:, :, 512 + o:512 + o + 256], wm[:, :, 512 + oo:512 + oo + 256]], axis=2))
        colv = np.zeros((L, 128, NCV), np.float32)
        colv[:, :, 0] = np.tile(np.asarray(inputs["swa_q_gain"]), (1, 2))
        colv[:, :, 1] = np.tile(np.asarray(inputs["swa_k_gain"]), (1, 2))
        colv[:, :, 2] = np.asarray(inputs["mem_q_gain"])
        for c in range(2):
            gc = 2 * r + c
            for k in range(4):
                colv[:, :, 3 + 4 * k + c] = cw[:, k, gc * 128:(gc + 1) * 128]
            colv[:, :, 19 + c] = np.asarray(inputs["conv_b"])[:, gc * 128:(gc + 1) * 128]
            colv[:, :, 23 + c] = np.asarray(inputs["lru_b_a"])[:, gc * 128:(gc + 1) * 128]
            colv[:, :, 27 + c] = np.asarray(inputs["lru_b_x"])[:, gc * 128:(gc + 1) * 128]
            colv[:, :, 31 + c] = np.asarray(inputs["lru_lambda"])[:, gc * 128:(gc + 1) * 128]
        p["colv"] = colv
        rowv = np.zeros((L, 128, NRV), np.float32)
        rowv[:, :, 0:4] = np.asarray(inputs["swa_sinks"])[:, None, 4 * r:4 * r + 4]
        rowv[:, :, 8:136] = np.asarray(inputs["mem_k_gain"])[:, None, :]
        p["rowv"] = rowv
        lw = np.zeros((L, 2, 4, 128, 128), np.float32)
        for a, nm in enumerate(("lru_w_a", "lru_w_x")):
            wa = np.asarray(inputs[nm])
            for c in range(2):
                gc = 2 * r + c
                lw[:, a, c, 0:64, 0:64] = wa[:, 2 * gc]
                lw[:, a, c, 64:128, 64:128] = wa[:, 2 * gc + 1]
        p["lruw"] = lw
    return shared, per


_NC_CACHE = {}


def run(inputs, S, L, ncores):
    key = (S, L, ncores)
    if key not in _NC_CACHE:
        _NC_CACHE[key] = build(S, L, ncores)
    nc = _NC_CACHE[key]
    shared, per = _prep(inputs, L)
    x = np.asarray(inputs["x"], dtype=np.float32)
    mem = np.asarray(inputs["mem"], dtype=np.float32)
    in_maps = []
    for c in range(ncores):
        b, r = c // 2, c % 2
        m = dict(shared)
        m.update(per[r])
        m["x"] = np.ascontiguousarray(x[b])
        m["x_own"] = np.ascontiguousarray(x[b].reshape(S // 1024, 2, 512, 1024)[:, r].reshape(S // 2, 1024))
        m["rk"] = np.array([[r, 1 - r]], dtype=np.int32)
        m["mem"] = np.ascontiguousarray(mem[b])
        in_maps.append(m)
    res = run_bass_kernel_spmd(nc, in_maps, core_ids=list(range(ncores)))
    return [r["y"] for r in res.results]


def kernel(**inputs):
    outs = run(inputs, 8192, 2, 8)
    return np.stack([_merge(outs[2 * b], outs[2 * b + 1]) for b in range(4)], axis=0).astype(np.float32)


def _merge(y0, y1):
    n = y0.shape[0] // 512
    return np.stack([y0.reshape(n, 512, -1), y1.reshape(n, 512, -1)], axis=1).reshape(2 * y0.shape[0], -1)
```

```python
import numpy as np
from contextlib import ExitStack
import concourse.bass as bass
import concourse.mybir as mybir
from concourse.bass_utils import run_bass_kernel_spmd

F32 = mybir.dt.float32
BF16 = mybir.dt.bfloat16
AF = mybir.ActivationFunctionType
ALU = mybir.AluOpType
EPS = 1e-6
NEG = -30000.0


class Res:
    __slots__ = ("name", "w", "r")

    def __init__(self, name):
        self.name = name
        self.w = None
        self.r = {}


class TT:
    __slots__ = ("t", "res", "name")

    def __init__(self, t, name):
        self.t = t
        self.name = name
        self.res = Res(name)

    def __getitem__(self, k):
        return self.t[k]


class Prog:
    def __init__(self):
        self.streams = {"pe": [], "act": [], "dve": [], "pool": [], "sp": []}
        self.count = {}
        self.seen = {e: {} for e in self.streams}
        self.chans = []

    def op(self, eng, fn, R=(), W=(), chan=None, inc=None):
        dma = chan is not None
        ch = chan if dma else eng
        if ch not in self.count:
            self.count[ch] = 0
            self.chans.append(ch)
        need = {}

        def add(c, n):
            if need.get(c, 0) < n:
                need[c] = n
        for r in R:
            if r.w is not None:
                add(*r.w)
        for w in W:
            if w.w is not None:
                add(*w.w)
            for c, n in w.r.items():
                add(c, n)
        seen = self.seen[eng]
        waits = []
        for c, n in need.items():
            if c == "pe" and eng == "pe":
                continue
            if seen.get(c, 0) >= n:
                continue
            seen[c] = n
            waits.append((c, n))
        if inc is None:
            inc = 16 if dma else 1
        self.count[ch] += inc
        my = self.count[ch]
        for r in R:
            if r.r.get(ch, 0) < my:
                r.r[ch] = my
        for w in W:
            w.w = (ch, my)
            w.r = {}
        self.streams[eng].append((waits, fn, ch, inc))

    def raw(self, eng, fn):
        self.streams[eng].append(([], fn, None, 0))

    def barrier(self):
        waits = [(c, n) for c, n in self.count.items() if n > 0]
        for eng in self.streams:
            self.streams[eng].append((list(waits), None, None, 0))
            for c, n in waits:
                if self.seen[eng].get(c, 0) < n:
                    self.seen[eng][c] = n

    def mm(self, out, lhsT, rhs, start, stop, R, W):
        self.op("pe", lambda e: e.matmul(out, lhsT, rhs, start=start, stop=stop), R, W)

    def tr(self, out, in_, ident, R, W):
        self.op("pe", lambda e: e.transpose(out, in_, ident), R, W)

    def act(self, out, in_, func, R, W, scale=1.0, bias=None, accum=None):
        kw = {}
        if bias is not None:
            kw["bias"] = bias
        if accum is not None:
            kw["accum_out"] = accum
        self.op("act", lambda e: e.activation(out=out, in_=in_, func=func, scale=scale, **kw), R, W)

    def tt(self, eng, out, a, b, op, R, W):
        self.op(eng, lambda e: e.tensor_tensor(out=out, in0=a, in1=b, op=op), R, W)

    def ts(self, eng, out, a, s1, s2, op0, op1, R, W):
        self.op(eng, lambda e: e.tensor_scalar(out=out, in0=a, scalar1=s1, scalar2=s2, op0=op0, op1=op1), R, W)

    def stt(self, eng, out, a, s, b, op0, op1, R, W):
        self.op(eng, lambda e: e.scalar_tensor_tensor(out=out, in0=a, scalar=s, in1=b, op0=op0, op1=op1), R, W)

    def cp(self, eng, out, in_, R, W):
        if eng == "act":
            self.op("act", lambda e: e.activation(out=out, in_=in_, func=AF.Copy), R, W)
        else:
            self.op(eng, lambda e: e.tensor_copy(out=out, in_=in_), R, W)

    def memset(self, eng, ap, val, W):
        self.op(eng, lambda e: e.memset(ap, val), (), W)

    def recip(self, out, in_, R, W):
        self.op("dve", lambda e: e.reciprocal(out=out, in_=in_), R, W)

    def dma(self, q, out, in_, R, W, chan):
        self.op(q, lambda e: e.dma_start(out=out, in_=in_), R, W, chan=chan)


O_AQ, O_AK, O_AV, O_AG = 0, 512, 640, 768
O_BQ, O_BK, O_BV, O_BG = 1280, 1792, 2304, 2816
O_CX, O_CG, O_MQ, O_MG, O_MERGE = 3328, 3840, 4352, 4864, 5376
IN_W = 9472
NCV = 35
NRV = 8 + 128


def build(S, L, NCORES):
    RG = [[2 * k, 2 * k + 1] for k in range(NCORES // 2)]
    NT = S // 512
    NB = S // 128
    nc = bass.Bass("TRN2", target_bir_lowering=False)
    P = Prog()
    es = ExitStack()

    def din(name, shape, dt=F32):
        return nc.dram_tensor(name, shape, dt, kind="ExternalInput").ap()

    NTH = NT // 2
    x_in = din("x", [S, 1024])
    x_own = din("x_own", [S // 2, 1024])
    rk = nc.dram_tensor("rk", [1, 2], mybir.dt.int32, kind="ExternalInput").ap()
    mem_in = din("mem", [256, 1024])
    w_c = din("w_c", [L, 1024, 6144])
    w_sb = din("w_sb", [L, 1024, 768])
    w_mem = din("w_mem_kv", [L, 1024, 1024])
    w_br = din("w_branch", [L, 4, 512, 1024])
    w_out = din("w_out", [L, 1024, 1024])
    gain_bc = din("gain_bc", [L, 128, 1024])
    mgain_bc = din("mgain_bc", [L, 128, 1024])
    colv = din("colv", [L, 128, NCV])
    rowv = din("rowv", [L, 128, NRV])
    lruw = din("lruw", [L, 2, 4, 128, 128])
    consts = din("consts", [128, 8, 128])
    y_out = nc.dram_tensor("y", [S // 2, 1024], F32, kind="ExternalOutput").ap()

    def dscr(name, shape, dt=BF16):
        return nc.dram_tensor(name, shape, dt).ap()

    WSB = [dscr(f"WSB{l}", [128, 8, 768]) for l in range(L)]
    WC = [dscr(f"WC{l}", [128, 8 * 6144]) for l in range(L)]
    WBR = [dscr(f"WBR{l}", [128, 8, 4, 4, 128]) for l in range(L)]
    WOUT = [dscr(f"WOUT{l}", [128, 8 * 1024]) for l in range(L)]
    WMEM = [dscr(f"WMEM{l}", [128, 8, 1024]) for l in range(L)]
    UT = dscr("UT", [128, 8, S])
    QZ = dscr("QZ", [128, 4, S])
    KT = dscr("KT", [128, 2, S])
    VS = dscr("VS", [NB, 128, 256])
    NPC = max(1, NB // 32)
    NBP = NB // NPC
    YBH = [[dscr(f"YBH{l}_{q}", [NBP * 128, 256]) for q in range(NPC)] for l in range(L)]
    YBF = [[dscr(f"YBF{l}_{q}", [2 * NBP * 128, 256]) for q in range(NPC)] for l in range(L)]
    RYBF = [Res(f"rybf{l}") for l in range(L)]
    X1HL = [[dscr(f"X1H{q}_{t}", [512, 1024], F32) for t in range(NTH)] for q in range(2)]
    X1F = [dscr(f"X1F{t}", [1024, 1024], F32) for t in range(NTH)]
    YTSH = [dscr(f"YTSH{p}", [2 * 128 * 8, 512]) for p in range(NTH)]
    YTSF = [dscr(f"YTSF{p}", [2 * 2 * 128 * 8, 512]) for p in range(NTH)]
    RYTS = [Res(f"ryts{i}") for i in range(NT)]
    RYTSF = [Res(f"rytsf{p}") for p in range(NTH)]
    RWSB = [Res(f"rwsb{l}") for l in range(L)]
    RWC = [Res(f"rwc{l}") for l in range(L)]
    RWBR = [Res(f"rwbr{l}") for l in range(L)]
    RWOUT = [Res(f"rwout{l}") for l in range(L)]
    RWMEM = [Res(f"rwmem{l}") for l in range(L)]
    RUT = [Res(f"rut{i}") for i in range(NT)]
    RQZ = [Res(f"rqz{i}") for i in range(NT)]
    RKT = [Res(f"rkt{i}") for i in range(NT)]
    RVS = [Res(f"rvs{i}") for i in range(NT)]
    RYB = [Res(f"ryb{i}") for i in range(NT)]
    RX1HL = [[Res(f"rx1h{q}_{t}") for t in range(NTH)] for q in range(2)]
    RX1F = [Res(f"rx1f{t}") for t in range(NTH)]
    rkv = {}

    def load_rank(e):
        reg = e.alloc_register("rkreg")
        ins = e.reg_load(reg, rk[0:1, 0:1])
        rkv["r"] = e.snap(reg, min_val=0, max_val=1)
        return ins
    P.raw("sp", load_rank)

    def sb(name, shape, dt=F32):
        return TT(es.enter_context(nc.sbuf_tensor(name, shape, dt)), name)

    ARENA_W = 48 * 1024
    arena = es.enter_context(nc.sbuf_tensor("arena", [128, ARENA_W], F32))
    car = [0]

    def carve(name, shape, dt=F32):
        n = int(np.prod(shape[1:]))
        nw = n if dt == F32 else (n + 1) // 2
        nw = (nw + 7) // 8 * 8
        ap = arena[:, car[0]:car[0] + nw]
        car[0] += nw
        assert car[0] <= ARENA_W, (name, car[0])
        if dt == BF16:
            ap = ap.bitcast(BF16)[:, 0:n]
        else:
            ap = ap[:, 0:n]
        if len(shape) == 3:
            ap = ap.rearrange("p (a b) -> p a b", a=shape[1])
        elif len(shape) == 4:
            ap = ap.rearrange("p (a b c) -> p a b c", a=shape[1], b=shape[2])
        return TT(ap, name)

    def phase():
        P.barrier()
        car[0] = 0

    psall = es.enter_context(nc.psum_tensor("psall", [128, 8, 512], F32))
    banks = [TT(psall[:, i, :], f"bank{i}") for i in range(8)]
    bank_ctr = [0]

    def nb():
        b = banks[bank_ctr[0] % 8]
        bank_ctr[0] += 1
        return b

    cst_f = carve("cst_f", [128, 8, 128])
    cst = sb("cst", [128, 8, 128], BF16)
    P.dma("sp", cst_f[:], consts, [], [cst_f.res], "c_cst")
    P.cp("dve", cst[:], cst_f[:], [cst_f.res], [cst.res])
    ident = cst[:, 0, :]
    negtri = cst[:, 1, :]
    negones = cst[:, 2, :]
    m_sb = cst[:, 3, :]
    m_cur = cst[:, 4, :]
    m_prev = cst[:, 5, :]
    bones64 = cst[:, 6, :]
    ones128 = cst[:, 7, :]
    mask4 = sb("mask4", [128, 2, 4, 128], BF16)
    for q in range(4):
        P.cp("dve", mask4[:, 0, q, :], cst_f[:, 5, :], [cst_f.res], [mask4.res])
        P.cp("dve", mask4[:, 1, q, :], cst_f[:, 4, :], [cst_f.res], [mask4.res])
    CR = [cst.res]

    def wcast(dst, src_rows_cols, res, chan):
        P.dma("pool", dst, src_rows_cols.rearrange("(k p) c -> p k c", p=128), [], [res], chan)

    for l in range(L):
        wcast(WSB[l], w_sb[l], RWSB[l], f"c_wsb{l}")
        for c0, n_ in [(0, 512), (512, 256), (768, 256), (1024, 512), (1536, 512)] + [(2048 + 512 * d_, 512) for d_ in range(8)]:
            wcast(WC[l][:, 8 * c0:8 * (c0 + n_)].rearrange("p (k c) -> p k c", k=8), w_c[l][:, c0:c0 + n_], RWC[l], f"c_wc{l}")
        for n in range(4):
            for d in range(8):
                P.dma("pool", WBR[l][:, d, n, :, :],
                      w_br[l, n, :, d * 128:(d + 1) * 128].rearrange("(k p) c -> p k c", p=128),
                      [], [RWBR[l]], f"c_wbr{l}")
        for c0 in (0, 512):
            wcast(WOUT[l][:, 8 * c0:8 * (c0 + 512)].rearrange("p (k c) -> p k c", k=8), w_out[l][:, c0:c0 + 512], RWOUT[l], f"c_wout{l}")
        wcast(WMEM[l], w_mem[l], RWMEM[l], f"c_wmem{l}")

    cv = sb("cv", [128, NCV])
    rv = sb("rv", [128, NRV])
    cvx = sb("cvx", [128, 8])
    cvh = sb("cvh", [128, 12])
    exps = sb("exps", [128, 8])
    lwb = sb("lwb", [128, 2, 4, 128], BF16)
    SS = [sb(f"ss{i}", [128, 4]) for i in range(4)]
    mkT = sb("mkT", [128, 4, 256], BF16)
    mva = sb("mva", [128, 2, 4, 129], BF16)
    P.memset("pool", mva[:], 1.0, [mva.res])
    epsb = sb("epsb", [128, 2])
    P.memset("dve", epsb[:, 0:1], EPS, [epsb.res])
    P.memset("dve", epsb[:, 1:2], 1.0, [epsb.res])
    junkh = [None]

    def rms_rows(xs, gain_tile, out_bf, sst):
        junk = junkh[0]
        P.act(junk[:], xs[:], AF.Square, [xs.res], [junk.res, sst.res], accum=sst[:, 0:1])
        P.act(sst[:, 1:2], sst[:, 0:1], AF.Sqrt, [sst.res], [sst.res], scale=1.0 / 1024, bias=epsb[:, 0:1])
        P.recip(sst[:, 2:3], sst[:, 1:2], [sst.res], [sst.res])
        P.stt("dve", out_bf[:], xs[:], sst[:, 2:3], gain_tile[:], ALU.mult, ALU.mult,
              [xs.res, sst.res, gain_tile.res], [out_bf.res])

    def transpose_to(src_bf, dst_ap_fn, nchunk, evac_eng):
        b = nb()
        bv = b.t[:].bitcast(BF16)
        for c in range(nchunk):
            P.tr(bv[:, c * 128:(c + 1) * 128], src_bf[:, c * 128:(c + 1) * 128], ident, [src_bf.res] + CR, [b.res])
        return b, bv

    for l in range(L):
        phase()
        gbc = carve("gbc", [128, 1024])
        lw = carve("lw", [128, 2, 4, 128])
        wbig = carve("wbig", [128, 8, 1024], BF16)
        XS = [carve(f"xs{i}", [128, 1024]) for i in range(4)]
        junk = carve("junk", [128, 1024], BF16)
        junkh[0] = junk
        ubf = [carve(f"ubf{i}", [128, 1024], BF16) for i in range(2)]
        ubf4 = ubf + [carve(f"ubf{i}", [128, 1024], BF16) for i in range(2, 4)]
        UTB = [carve(f"utb{i}", [128, 8, 512], BF16) for i in range(2)]
        QZT = [carve(f"qzt{i}", [128, 4, 512], BF16) for i in range(2)]
        KTT = [carve(f"ktt{i}", [128, 2, 512], BF16) for i in range(2)]
        VTT = [carve(f"vtt{i}", [128, 4, 256], BF16) for i in range(2)]
        mnT = carve("mnT", [128, 8, 256], BF16)
        mkn = carve("mkn", [128, 512], BF16)
        for q in QZT:
            P.memset("pool", q[:], 0.0, [q.res])
        P.dma("sp", gbc[:], gain_bc[l], [], [gbc.res], "c_gbc")
        P.dma("sp", cv[:], colv[l], [], [cv.res], "c_cv")
        P.dma("sp", rv[:], rowv[l], [], [rv.res], "c_rv")
        P.dma("sp", lw[:], lruw[l].rearrange("a c p j -> p a c j"), [], [lw.res], "c_lw")
        P.cp("dve", lwb[:], lw[:], [lw.res], [lwb.res])
        P.ts("dve", cvx[:, 0:1], cv[:, 0:1], 0.125, None, ALU.mult, ALU.bypass, [cv.res], [cvx.res])
        P.ts("dve", cvx[:, 1:2], cv[:, 2:3], float(128 ** -0.5), None, ALU.mult, ALU.bypass, [cv.res], [cvx.res])
        P.act(cvx[:, 2:6], cv[:, 31:35], AF.Exp, [cv.res], [cvx.res], scale=-1.0)
        P.act(cvx[:, 2:6], cvx[:, 2:6], AF.Ln, [cvx.res], [cvx.res], bias=epsb[:, 1:2])
        P.ts("dve", cvx[:, 2:6], cvx[:, 2:6], -8.0, None, ALU.mult, ALU.bypass, [cvx.res], [cvx.res])
        P.act(exps[:], rv[:, 0:8], AF.Exp, [rv.res], [exps.res])
        P.ts("dve", cvh[:, 0:8], cv[:, 23:31], 0.5, None, ALU.mult, ALU.bypass, [cv.res], [cvh.res])
        P.ts("dve", cvh[:, 8:12], cvx[:, 2:6], 0.5, None, ALU.mult, ALU.bypass, [cvx.res], [cvh.res])

        P.dma("sp", wbig[:, :, 0:1024], WMEM[l], [RWMEM[l]], [wbig.res], "c_wbig")
        mg = XS[3]
        P.dma("sp", mg[:], mgain_bc[l], [], [mg.res], "c_xs3")
        for r in range(2):
            xs = XS[r]
            P.dma("sp", xs[:], mem_in[r * 128:(r + 1) * 128, :], [], [xs.res], f"c_xs{r}")
            rms_rows(xs, mg, ubf[r], SS[r])
            b, bv = transpose_to(ubf[r], None, 8, "act")
            P.cp("act", mnT[:, :, r * 128:(r + 1) * 128], bv.rearrange("p (c t) -> p c t", c=8), [b.res], [mnT.res])
        for r in range(2):
            b = nb()
            for k in range(8):
                P.mm(b[:], mnT[:, k, r * 128:(r + 1) * 128], wbig[:, k, 0:512], k == 0, k == 7, [mnT.res, wbig.res], [b.res])
            sst = SS[2 + r]
            for hm in range(4):
                P.act(junk[:, 0:128], b[:, hm * 128:(hm + 1) * 128], AF.Square, [b.res], [junk.res, sst.res],
                      accum=sst[:, hm:hm + 1])
            P.act(sst[:], sst[:], AF.Sqrt, [sst.res], [sst.res], scale=1.0 / 128, bias=epsb[:, 0:1])
            P.recip(sst[:], sst[:], [sst.res], [sst.res])
            for hm in range(4):
                P.stt("dve", mkn[:, hm * 128:(hm + 1) * 128], b[:, hm * 128:(hm + 1) * 128], sst[:, hm:hm + 1],
                      rv[:, 8:136], ALU.mult, ALU.mult, [b.res, sst.res, rv.res], [mkn.res])
            b2, bv2 = transpose_to(mkn, None, 4, "act")
            P.cp("act", mkT[:, :, r * 128:(r + 1) * 128], bv2[:, 0:512].rearrange("p (c t) -> p c t", c=4), [b2.res], [mkT.res])
            b = nb()
            for k in range(8):
                P.mm(b[:], mnT[:, k, r * 128:(r + 1) * 128], wbig[:, k, 512:1024], k == 0, k == 7, [mnT.res, wbig.res], [b.res])
            P.cp("act", mva[:, r, :, 0:128], b[:].rearrange("p (h d) -> p h d", h=4), [b.res], [mva.res])

        P.dma("sp", wbig[:, :, 0:768], WSB[l], [RWSB[l]], [wbig.res], "c_wbig")

        def gen_norm(i, j):
            uT = UTB[i % 2]
            row0 = i * 512 + j * 128
            xs = XS[j]
            if l == 0:
                P.dma("sp", xs[:], x_in[row0:row0 + 128, :], [], [xs.res], f"c_xs{j}")
            else:
                xr0 = (i % 2) * 512 + j * 128
                P.dma("sp", xs[:], X1F[i // 2][xr0:xr0 + 128, :], [RX1F[i // 2]], [xs.res], f"c_xs{j}")
            yield
            u = ubf4[j]
            rms_rows(xs, gbc, u, SS[j])
            yield
            b, bv = transpose_to(u, None, 8, "act")
            P.cp("act" if j % 2 == 0 else "dve", uT[:, :, j * 128:(j + 1) * 128],
                 bv.rearrange("p (c t) -> p c t", c=8), [b.res], [uT.res])

        def gen_sb(i):
            uT = UTB[i % 2]
            qz, kt, vt = QZT[i % 2], KTT[i % 2], VTT[i % 2]
            for c in range(2):
                yield
                b = nb()
                for k in range(8):
                    P.mm(b[:], wbig[:, k, c * 128:(c + 1) * 128], uT[:, k, :], k == 0, k == 7, [wbig.res, uT.res], [b.res])
                P.act(qz[0:64, 2 * c, :], b[0:64, :], AF.Copy, [b.res], [qz.res], scale=0.125)
                P.ts("dve", qz[64:128, 2 * c + 1, :], b[64:128, :], 0.125, None, ALU.mult, ALU.bypass, [b.res], [qz.res])
            for c in range(2):
                yield
                b = nb()
                for k in range(8):
                    P.mm(b[:], wbig[:, k, 256 + c * 128:256 + (c + 1) * 128], uT[:, k, :], k == 0, k == 7,
                         [wbig.res, uT.res], [b.res])
                P.cp("act" if c % 2 == 0 else "dve", kt[:, c, :], b[:], [b.res], [kt.res])
            for j in range(4):
                yield
                b = nb()
                for k in range(8):
                    P.mm(b[:, 0:256], uT[:, k, j * 128:(j + 1) * 128], wbig[:, k, 512:768], k == 0, k == 7,
                         [wbig.res, uT.res], [b.res])
                P.cp("act" if j % 2 == 0 else "dve", vt[:, j, :], b[:, 0:256], [b.res], [vt.res])
            P.dma("pool", QZ[:, :, i * 512:(i + 1) * 512], qz[:], [qz.res], [RQZ[i]], f"c_qzt{i % 2}")
            P.dma("pool", KT[:, :, i * 512:(i + 1) * 512], kt[:], [kt.res], [RKT[i]], f"c_ktt{i % 2}")
            P.dma("pool", VS[4 * i:4 * i + 4].rearrange("j p c -> p j c"), vt[:], [vt.res], [RVS[i]], f"c_vtt{i % 2}")

        def run_il(gens):
            gens = list(gens)
            while gens:
                for gq in list(gens):
                    try:
                        next(gq)
                    except StopIteration:
                        gens.remove(gq)

        for i in range(NT):
            gl = [gen_norm(i, j) for j in range(4)]
            if i > 0:
                gl.append(gen_sb(i - 1))
            run_il(gl)
            P.dma("pool", UT[:, :, i * 512:(i + 1) * 512], UTB[i % 2][:], [UTB[i % 2].res], [RUT[i]], f"c_utb{i % 2}")
        run_il([gen_sb(NT - 1)])

        phase()
        ybh_views = [t.rearrange("(b p) c -> b p c", p=128) for t in YBH[l]]
        pass_B(P, carve, l, S, NT, NB, banks, psall, QZ, KT, VS, ybh_views, NBP, RQZ, RKT, RVS, RYB, cst, CR)

        phase()
        pass_C(P, carve, l, L, S, NT, locals())

    fin = [(c, n) for c, n in P.count.items() if c.startswith("c_")]
    P.streams["pool"].append((fin, None, None, 0))

    sems = {c: es.enter_context(nc.semaphore("s_" + c)) for c in P.chans}
    block = es.enter_context(nc.Block())

    def emit(stream):
        def f(eng):
            for waits, fn, ch, inc in stream:
                for c, n in waits:
                    eng.wait_ge(sems[c], n)
                if fn is not None:
                    r = fn(eng)
                    if ch is not None:
                        r.then_inc(sems[ch], inc)
        return f

    block.tensor(emit(P.streams["pe"]))
    block.scalar(emit(P.streams["act"]))
    block.vector(emit(P.streams["dve"]))
    block.gpsimd(emit(P.streams["pool"]))
    block.sync(emit(P.streams["sp"]))
    es.close()
    return nc


def pass_B(P, carve, l, S, NT, NB, banks, psall, QZ, KT, VS, YBV, NBP, RQZ, RKT, RVS, RYB, cst, CR):
    KB = 32
    NS = KB
    NR = KB + 8
    qzp = carve("b_qzp", [128, 2, S], BF16)
    ktp = carve("b_ktp", [128, S], BF16)
    vp = carve("b_vp", [128, NB, 128], BF16)
    LBt = carve("b_lall", [128, NS, 512], BF16)
    LB = [TT(LBt[:, i, :], f"b_l{i}") for i in range(NS)]
    RBT = [carve(f"b_r{i}", [128, 512], BF16) for i in range(NR)]
    ABt = carve("b_aall", [128, NS, 512], BF16)
    AB = [TT(ABt[:, i, :], f"b_a{i}") for i in range(NS)]
    YT = [carve(f"b_y{i}", [128, 4, 128], BF16) for i in range(2)]
    ident, negtri, negones, m_sb = cst[:, 0, :], cst[:, 1, :], cst[:, 2, :], cst[:, 3, :]
    ZP = [(0, 1), (2, 3)]
    ACC = [[banks[4], banks[5]], [banks[6], banks[7]]]
    NCH = min(4, NT)
    CT = S // NCH
    qres = [Res(f"b_qres{c}") for c in range(NCH)]
    kres = [Res(f"b_kres{c}") for c in range(NCH)]
    vres = [Res(f"b_vres{c}") for c in range(NCH)]
    for hp in range(2):
        for c in range(NCH):
            t0, t1 = c * CT, (c + 1) * CT
            P.dma("sp", qzp[:, :, t0:t1], QZ[:, 2 * hp:2 * hp + 2, t0:t1], RQZ, [qres[c]], f"c_qzp{c}")
            P.dma("sp", ktp[:, t0:t1], KT[:, hp, t0:t1], RKT, [kres[c]], f"c_ktp{c}")
            b0, b1 = t0 // 128, t1 // 128
            P.dma("sp", vp[:, b0:b1, :], VS[b0:b1, :, hp * 128:(hp + 1) * 128].rearrange("b p c -> p b c"),
                  RVS, [vres[c]], f"c_vp{c}")
        items = []
        for i in range(NT):
            nblk = 4 * i + 4
            for n, kb in enumerate(reversed(range(nblk))):
                for s in range(2):
                    jd = kb - 4 * i if kb >= 4 * i else -1
                    items.append(dict(s=s, i=i, kb=kb, n=n, jd=jd, c0=(128 * jd if jd >= 0 else 0),
                                      first=(n == 0), last=(n == nblk - 1)))
        NI = len(items)

        def zmm(w, zb):
            it = items[w]
            c0, s, i, kb = it["c0"], it["s"], it["i"], it["kb"]
            diag = it["jd"] >= 0
            P.mm(zb[:, c0:512], ktp[:, kb * 128:(kb + 1) * 128], qzp[:, s, i * 512 + c0:(i + 1) * 512],
                 True, False, [kres[(kb * 128) // CT], qres[(i * 512) // CT]], [zb.res])
            if diag:
                P.mm(zb[:, c0:c0 + 128], ident, m_sb, False, False, CR, [zb.res])

        def rupd(w):
            it = items[w]
            lt = LB[w % NS]
            c0 = it["c0"]
            if not it["last"]:
                rn = RBT[(w + 2) % NR]
                if it["first"]:
                    if c0 > 0:
                        P.memset("pool", rn[:, 0:c0], 0.0, [rn.res])
                    P.cp("pool", rn[:, c0:512], lt[:, c0:512], [lt.res], [rn.res])
                else:
                    ro = RBT[w % NR]
                    if c0 > 0:
                        P.cp("pool", rn[:, 0:c0], ro[:, 0:c0], [ro.res], [rn.res])
                    P.tt("dve", rn[:, c0:512], ro[:, c0:512], lt[:, c0:512], ALU.add, [ro.res, lt.res], [rn.res])

        def P1(w):
            it = items[w]
            c0 = it["c0"]
            bp = ZP[(w // 2) % 2]
            zmm(w, banks[bp[0]])
            zmm(w + 1, banks[bp[1]])
            s0 = w % NS
            P.act(LBt[:, s0:s0 + 2, c0:512], psall[:, bp[0]:bp[0] + 2, c0:512], AF.Softplus,
                  [banks[bp[0]].res, banks[bp[1]].res], [LB[s0].res, LB[s0 + 1].res])
            rupd(w)
            rupd(w + 1)

        def P2a(w):
            bp = ZP[(w // 2) % 2]
            for q in range(2):
                it = items[w + q]
                zb, lt = banks[bp[q]], LB[(w + q) % NS]
                c0 = it["c0"]
                zmm(w + q, zb)
                P.mm(zb[:, c0:512], negtri, lt[:, c0:512], False, it["first"], [lt.res] + CR, [zb.res])
                if not it["first"]:
                    ro = RBT[(w + q) % NR]
                    P.mm(zb[:, c0:512], negones, ro[:, c0:512], False, True, [ro.res] + CR, [zb.res])

        def P2b(w):
            it = items[w]
            c0 = it["c0"]
            bp = ZP[(w // 2) % 2]
            s0 = w % NS
            P.act(ABt[:, s0:s0 + 2, c0:512], psall[:, bp[0]:bp[0] + 2, c0:512], AF.Exp,
                  [banks[bp[0]].res, banks[bp[1]].res], [AB[s0].res, AB[s0 + 1].res])

        def P2c(w):
            it = items[w]
            a = AB[w % NS]
            s, i, kb = it["s"], it["i"], it["kb"]
            acc = ACC[s][i % 2]
            j0 = max(it["jd"], 0)
            for j in range(j0, 4):
                P.mm(acc[:, j * 64:(j + 1) * 64], a[:, j * 128:(j + 1) * 128], vp[:, kb, s * 64:(s + 1) * 64],
                     it["first"] and j == j0, it["last"] and j == 3, [a.res, vres[(kb * 128) // CT]], [acc.res])
            if it["last"]:
                yt = YT[i % 2]
                P.cp("dve", yt[:, :, s * 64:(s + 1) * 64], acc[:, 0:256].rearrange("p (j d) -> p j d", j=4),
                     [acc.res], [yt.res])
                if s == 1:
                    ybq, ybo = YBV[(4 * i) // NBP], (4 * i) % NBP
                    P.dma("pool", ybq[ybo:ybo + 4, :, hp * 128:(hp + 1) * 128].rearrange("j p c -> p j c"), yt[:],
                          [yt.res], [RYB[i]], f"c_byt{i % 2}")

        nxt = 0
        for b0 in range(0, NI, KB):
            b1 = min(NI, b0 + KB)
            for w in range(b0, b1, 2):
                P1(w)
                for _ in range(2):
                    if nxt < b0:
                        P2c(nxt)
                        nxt += 1
            while nxt < b0:
                P2c(nxt)
                nxt += 1
            P2a(b0)
            for w in range(b0, b1, 2):
                if w + 2 < b1:
                    P2a(w + 2)
                P2b(w)
        while nxt < NI:
            P2c(nxt)
            nxt += 1


def pass_C(P, carve, l, L, S, NT, env):
    g = env
    banks = g["banks"]
    nbc = [0]

    def nb():
        b = banks[nbc[0] % 6]
        nbc[0] += 1
        return b
    cst, CR = g["cst"], g["CR"]
    ident, bones64, ones128 = cst[:, 0, :], cst[:, 6, :], cst[:, 7, :]
    mask4 = g["mask4"]
    cv, rv, cvx, exps, lwb, epsb, cvh = g["cv"], g["rv"], g["cvx"], g["exps"], g["lwb"], g["epsb"], g["cvh"]
    mkT, mva = g["mkT"], g["mva"]
    UT, WC, WBR, WOUT = g["UT"], g["WC"], g["WBR"], g["WOUT"]
    RUT, RYB, RWC, RWBR, RWOUT = g["RUT"], g["RYB"], g["RWC"], g["RWBR"], g["RWOUT"]
    NTH, rkv, YTSH, YTSF, RYTS, RYTSF = g["NTH"], g["rkv"], g["YTSH"], g["YTSF"], g["RYTS"], g["RYTSF"]
    YBHv, NBP = g["ybh_views"], g["NBP"]
    X1F, RX1F, x_own, y_out, RG = g["X1F"], g["RX1F"], g["x_own"], g["y_out"], g["RG"]
    X1H, RX1H = g["X1HL"][(l + 1) % 2], g["RX1HL"][(l + 1) % 2]
    XN, RXN = g["X1HL"][l % 2], g["RX1HL"][l % 2]
    _PC = {"ctr": 0, "mctr": 0, "oactr": 0}
    uTd = [carve(f"c_ut{i}", [128, 8, 512], BF16) for i in range(2)]
    WY = carve("c_wy", [128, 8, 2048], BF16)
    uTx = carve("c_utx", [128, 8, 512], BF16)
    yTx = carve("c_ytx", [128, 4, 4, 512], BF16)
    XSc = [carve(f"c_xs{i}", [128, 512]) for i in range(2)]
    WCS = [None, None, None] + [carve(f"c_wcs{i}", [128, 8, 512], BF16) for i in range(3, 5)]
    WBS = [carve(f"c_wbs{i}", [128, 4, 4, 128], BF16) for i in range(2)]
    qa = [carve(f"c_qa{h}", [128, 2, 512], BF16) for h in range(2)]
    kn = [[carve(f"c_kn{a}{i}", [128, 512], BF16) for i in range(2)] for a in range(1)]
    va = [carve(f"c_va{i}", [128, 4, 2, 65], BF16) for i in range(2)]
    sq = [carve(f"c_sq{i}", [128, 512], BF16) for i in range(2)]
    rs = [carve(f"c_rs{i}", [128, 512]) for i in range(2)]
    sg = [carve(f"c_sg{n}", [128, 4, 256], BF16) for n in range(3)]
    ysb = sg
    ybt = carve("c_ybt", [128, 4, 256], BF16)
    yTall = carve("c_yTall", [128, 8, 512], BF16)

    class _V:
        def __init__(self, t, n):
            self.t, self.n, self.res = t, n, t.res

        def __getitem__(self, k):
            if isinstance(k[1], slice):
                s0 = 2 * self.n + (k[1].start or 0)
                s1 = 2 * self.n + (k[1].stop if k[1].stop is not None else 2)
                return self.t[(k[0], slice(s0, s1)) + tuple(k[2:])]
            return self.t[(k[0], 2 * self.n + k[1]) + tuple(k[2:])]
    yT = [_V(yTall, n) for n in range(4)]
    psb_s = [carve(f"c_ps{i}", [128, 512], BF16) for i in range(2)]
    psb_m = [carve(f"c_pm{i}", [128, 512], BF16) for i in range(3)]
    den = [carve(f"c_den{i}", [128, 8]) for i in range(4)]
    qmn = carve("c_qmn", [128, 2, 512], BF16)
    cx = carve("c_cx", [128, 2, 515])
    ht = [carve("c_ht0", [128, 512])] * 2
    hcar = carve("c_hcar", [128, 8])
    xc = [carve("c_xc0", [128, 512])] * 2
    xcb = [carve("c_xcb0", [128, 512], BF16)] * 2
    lt = [carve(f"c_lt{q}", [128, 512]) for q in range(4)]
    sgc = carve("c_sgc", [128, 2, 512], BF16)
    gs = [carve(f"c_gs{i}", [128, 512]) for i in range(2)]
    tq = [carve(f"c_tq{i}", [128, 512]) for i in range(2)]
    tqc = [0]

    def silu2(out_ap, out_res, b_ap, b_res=None):
        if b_res is None:
            b_ap, b_res = b_ap[:], b_ap.res
        n = b_ap.shape[-1]
        t = tq[tqc[0] % 2]
        tqc[0] += 1
        P.act(t[:, 0:n], b_ap, AF.Tanh, [b_res], [t.res], scale=0.5)
        P.stt("dve", out_ap, t[:, 0:n], 1.0, b_ap, ALU.add, ALU.mult, [t.res, b_res], [out_res])
    mx = [carve(f"c_mx{i}", [128, 512]) for i in range(3)]
    mixT = carve("c_mixT", [128, 8, 512], BF16)
    xo = [carve(f"c_xo{i}", [128, 512]) for i in range(2)]
    for q in qa:
        P.memset("pool", q[:], 0.0, [q.res])
    for q in va:
        P.memset("pool", q[:], 1.0, [q.res])
    P.memset("pool", cx[:], 0.0, [cx.res])
    P.memset("pool", hcar[:], 0.0, [hcar.res])

    class _W:
        def __init__(self, t, off, res):
            self.t, self.off, self.res = t, off, res

        def __getitem__(self, k):
            s = k[2]
            return self.t[(k[0], k[1], slice(self.off + s.start, self.off + s.stop))]

    wyres = {}
    for c0_, n_ in [(0, 512), (768, 256), (1536, 512), (512, 256), (1024, 512)]:
        wyres[c0_] = Res(f"c_wyres{c0_}")
        P.dma("sp", WY[:, :, c0_:c0_ + n_], WC[l][:, 8 * c0_:8 * (c0_ + n_)].rearrange("p (k c) -> p k c", k=8),
              [RWC[l]], [wyres[c0_]], f"c_wy{c0_}")

    def load_wc(col0, ncols, slot, src=None, res=None):
        if slot < 3:
            return _W(WY, col0, wyres[col0])
        w = WCS[slot]
        if src is None:
            src, res = WC[l], RWC[l]
        P.dma("sp", w[:, :, 0:ncols], src[:, 8 * col0:8 * (col0 + ncols)].rearrange("p (k c) -> p k c", k=8),
              [res], [w.res], "c_" + w.name)
        return w

    def proj_fm(w, col0, uT):
        b = nb()
        for k in range(8):
            P.mm(b[:], w[:, k, col0:col0 + 128], uT[:, k, :], k == 0, k == 7, [w.res, uT.res], [b.res])
        return b

    def proj_tm(w, col0, ncol, uT, j):
        b = nb()
        for k in range(8):
            P.mm(b[:, 0:ncol], uT[:, k, j * 128:(j + 1) * 128], w[:, k, col0:col0 + ncol], k == 0, k == 7,
                 [w.res, uT.res], [b.res])
        return b

    tog = [0]

    def qknorm(b, onesmat, nfeat, writes):
        q = tog[0] % 2
        tog[0] += 1
        P.act(sq[q][:], b[:], AF.Square, [b.res], [sq[q].res])
        b2 = nb()
        P.mm(b2[:], onesmat, sq[q][:], True, True, [sq[q].res] + CR, [b2.res])
        P.act(rs[q][:], b2[:], AF.Sqrt, [b2.res], [rs[q].res], bias=epsb[:, 0:1])
        P.recip(rs[q][:], rs[q][:], [rs[q].res], [rs[q].res])
        return rs[q]

    UTv = UT.rearrange("p c (t h s) -> h t p c s", h=2, s=512)

    def stage_X(t):
        P.op("sp", lambda e: e.dma_start(out=uTx[:], in_=UTv[rkv["r"], t]),
             RUT, [uTx.res], chan="c_utx")
        ysf = YTSF[t].rearrange("(r h p n c) s -> r h p n c s", r=2, h=2, p=128, n=4)
        for sr in range(2):
            P.op("sp", lambda e, sr=sr: e.dma_start(out=yTx[:, :, 2 * sr:2 * sr + 2, :], in_=ysf[sr, rkv["r"]]),
                 [RYTSF[t]], [yTx.res], chan="c_ytx")
        xsrcs = [(2048 + d_ * 512, None, None) for d_ in range(8)] + [(ch_ * 512, WOUT[l], RWOUT[l]) for ch_ in range(2)]

        def xload(q):
            c0, s_, r_ = xsrcs[q]
            return load_wc(c0, 512, 3 + q % 2, s_, r_)
        wnext = xload(0)
        for d in range(8):
            yield
            w = wnext
            wnext = xload(d + 1)
            wb = WBS[d % 2]
            P.dma("sp", wb[:], WBR[l][:, d], [RWBR[l]], [wb.res], "c_" + wb.name)
            for n in range(4):
                yield
                bg = proj_fm(w, n * 128, uTx)
                gn = gs[n % 2]
                P.act(gn[:], bg[:], AF.Tanh, [bg.res], [gn.res], scale=0.5)
                bu = nb()
                for kc in range(4):
                    P.mm(bu[:], wb[:, n, kc, :], yTx[:, n, kc, :], kc == 0, kc == 3, [wb.res, yTx.res], [bu.res])
                if n == 0:
                    P.stt("dve", mx[0][:], gn[:], 1.0, bu[:], ALU.add, ALU.mult, [gn.res, bu.res], [mx[0].res])
                else:
                    tmp = mx[1 + n % 2]
                    P.stt("dve", tmp[:], gn[:], 1.0, bu[:], ALU.add, ALU.mult, [gn.res, bu.res], [tmp.res])
                    if n < 3:
                        P.tt("dve", mx[0][:], mx[0][:], tmp[:], ALU.add, [mx[0].res, tmp.res], [mx[0].res])
                    else:
                        P.tt("dve", mixT[:, d, :], mx[0][:], tmp[:], ALU.add, [mx[0].res, tmp.res], [mixT.res])
        for ch in range(2):
            yield
            w = wnext
            if ch == 0:
                wnext = xload(9)
            for j in range(4):
                row0 = t * 512 + j * 128
                xs = XSc[j % 2]
                if l == 0:
                    P.dma("sp", xs[:], x_own[row0:row0 + 128, ch * 512:(ch + 1) * 512], [], [xs.res], "c_" + xs.name)
                else:
                    P.dma("sp", xs[:], X1H[t][j * 128:(j + 1) * 128, ch * 512:(ch + 1) * 512], [RX1H[t]], [xs.res], "c_" + xs.name)
                yield
                b = nb()
                for k in range(8):
                    P.mm(b[:], mixT[:, k, j * 128:(j + 1) * 128], w[:, k, :], k == 0, k == 7, [mixT.res, w.res], [b.res])
                o = xo[j % 2]
                P.stt("dve", o[:], b[:], 0.5, xs[:], ALU.mult, ALU.add, [b.res, xs.res], [o.res])
                if l == L - 1:
                    P.dma("pool", y_out[row0:row0 + 128, ch * 512:(ch + 1) * 512], o[:], [o.res], [], "c_" + o.name)
                else:
                    P.dma("pool", XN[t][j * 128:(j + 1) * 128, ch * 512:(ch + 1) * 512], o[:], [o.res], [RXN[t]], "c_" + o.name)
        if l < L - 1:
            P.op("pool", lambda e: e.collective_compute("AllGather", ALU.bypass, RG, [XN[t].opt()], [X1F[t].opt()]),
                 [RXN[t]], [RX1F[t]], chan="c_cc", inc=1)

    def gen_swa(i):
        pi = i % 2
        uT = uTd[i % 2]
        yield
        w = load_wc(0, 512, 0)
        for c in range(2):
            yield
            b = proj_fm(w, c * 128, uT)
            r = qknorm(b, bones64, 64, None)
            P.stt("dve", qa[0][0:64, c, :], b[0:64, :], cvx[0:64, 0:1], r[0:64, :], ALU.mult, ALU.mult,
                  [b.res, r.res, cvx.res], [qa[0].res])
            P.stt("dve", qa[1][64:128, c, :], b[64:128, :], cvx[64:128, 0:1], r[64:128, :], ALU.mult, ALU.mult,
                  [b.res, r.res, cvx.res], [qa[1].res])
        for a in range(1):
            yield
            b = proj_fm(w, 256, uT)
            r = qknorm(b, bones64, 64, None)
            P.stt("dve", kn[a][pi][:], b[:], cv[:, 1:2], r[:], ALU.mult, ALU.mult, [b.res, r.res, cv.res], [kn[a][pi].res])
        for j in range(4):
            yield
            b = proj_tm(w, 384, 64, uT, j)
            P.cp("act", va[pi][:, j, 0, 0:64], b[:, 0:64], [b.res], [va[pi].res])
        yield
        w = load_wc(512, 256, 0)
        for j in range(4):
            yield
            b = proj_tm(w, 0, 256, uT, j)
            silu2(sg[0][:, j, 0:256], sg[0].res, b[:, 0:256], b.res)
        for j in range(4):
            nblk = 4 * i + j
            for gq in range(1):
                yield
                oa = banks[6 + _PC["oactr"] % 2]
                _PC["oactr"] += 1
                kbs = ([0] if nblk > 0 else []) + [1]
                for kk, which in enumerate(kbs):
                    if which == 1:
                        ksrc = lambda a: kn[a][pi][:, j * 128:(j + 1) * 128]
                        kres = [kn[0][pi].res]
                        vsrc, vres = va[pi][:, j, gq, :], va[pi].res
                    else:
                        if j > 0:
                            ksrc = lambda a: kn[a][pi][:, (j - 1) * 128:j * 128]
                            kres = [kn[0][pi].res]
                            vsrc, vres = va[pi][:, j - 1, gq, :], va[pi].res
                        else:
                            ksrc = lambda a: kn[a][1 - pi][:, 384:512]
                            kres = [kn[0][1 - pi].res]
                            vsrc, vres = va[1 - pi][:, 3, gq, :], va[1 - pi].res
                    yield
                    sc = nb()
                    for hq in range(4):
                        h = 4 * gq + hq
                        a = 0
                        P.mm(sc[:, hq * 128:(hq + 1) * 128], ksrc(a), qa[h % 2][:, h // 2, j * 128:(j + 1) * 128],
                             hq == 0, False, kres + [qa[0].res, qa[1].res], [sc.res])
                    P.mm(sc[:], ident, mask4[:, which, :, :].rearrange("p a b -> p (a b)"), False, True,
                         [mask4.res] + CR, [sc.res])
                    pt = psb_s[_PC["ctr"] % 2]
                    _PC["ctr"] += 1
                    P.act(pt[:], sc[:], AF.Exp, [sc.res], [pt.res])
                    for hq in range(4):
                        P.mm(oa[:, hq * 65:(hq + 1) * 65], pt[:, hq * 128:(hq + 1) * 128], vsrc,
                             kk == 0 and hq == 0, kk == len(kbs) - 1 and hq == 3, [pt.res, vres], [oa.res])
                dn = den[(2 * j + gq) % 4]
                oav = oa[:, 0:260].rearrange("p (h d) -> p h d", h=4)
                P.tt("dve", dn[:, 0:4], oav[:, :, 64], exps[:, 4 * gq:4 * gq + 4], ALU.add, [oa.res, exps.res], [dn.res])
                P.ts("dve", dn[:, 0:4], dn[:, 0:4], 2.0, None, ALU.mult, ALU.bypass, [dn.res], [dn.res])
                P.recip(dn[:, 4:8], dn[:, 0:4], [dn.res], [dn.res])
                for hq in range(4):
                    h = 4 * gq + hq
                    P.stt("dve", ysb[0][:, j, h * 64:(h + 1) * 64], oa[:, hq * 65:hq * 65 + 64], dn[:, 4 + hq:5 + hq],
                          sg[0][:, j, h * 64:(h + 1) * 64], ALU.mult, ALU.mult, [oa.res, dn.res, sg[0].res], [ysb[0].res])
        for n3, n in ((0, 0),):
            for j in range(4):
                yield
                b = nb()
                bv = b.t[:].bitcast(BF16)
                for c in range(2):
                    P.tr(bv[:, c * 128:(c + 1) * 128], ysb[n3][:, j, c * 128:(c + 1) * 128], ident, [ysb[n3].res] + CR, [b.res])
                P.cp("act" if j % 2 == 0 else "dve", yT[n][:, :, j * 128:(j + 1) * 128],
                     bv[:, 0:256].rearrange("p (c t) -> p c t", c=2), [b.res], [yT[n].res])

    def gen_mem(i):
        pi = i % 2
        uT = uTd[i % 2]
        yield
        w = load_wc(768, 256, 1)
        ybq, ybo = YBHv[(4 * i) // NBP], (4 * i) % NBP
        P.dma("sp", ybt[:], ybq[ybo:ybo + 4].rearrange("j p c -> p j c"), [RYB[i]], [ybt.res], "c_ybt")
        for j in range(4):
            yield
            b = proj_tm(w, 0, 256, uT, j)
            silu2(sg[1][:, j, 0:256], sg[1].res, b[:, 0:256], b.res)
            P.stt("dve", ysb[1][:, j, 0:256], ybt[:, j, :], 0.5, sg[1][:, j, 0:256], ALU.mult, ALU.mult, [ybt.res, sg[1].res], [ysb[1].res])
        yield
        w = load_wc(1024, 512, 1)
        for hm in range(2):
            yield
            b = proj_fm(w, hm * 128, uT)
            r = qknorm(b, ones128, 128, None)
            P.stt("dve", qmn[:, hm, :], b[:], cvx[:, 1:2], r[:], ALU.mult, ALU.mult, [b.res, r.res, cvx.res], [qmn.res])
        for j in range(4):
            yield
            b = proj_tm(w, 256, 256, uT, j)
            silu2(sg[2][:, j, 0:256], sg[2].res, b[:, 0:256], b.res)
        for hm in range(2):
            pts = []
            for mb in range(2):
                yield
                sc = nb()
                P.mm(sc[:], mkT[:, hm, mb * 128:(mb + 1) * 128], qmn[:, hm, :], True, True, [mkT.res, qmn.res], [sc.res])
                pt = psb_m[_PC["mctr"] % 3]
                _PC["mctr"] += 1
                P.act(pt[:], sc[:], AF.Exp, [sc.res], [pt.res])
                pts.append(pt)
            for jp in range(2):
                yield
                om = nb()
                for jj in range(2):
                    j = 2 * jp + jj
                    for mb in range(2):
                        P.mm(om[:, jj * 129:(jj + 1) * 129], pts[mb][:, j * 128:(j + 1) * 128], mva[:, mb, hm, :],
                             jj == 0 and mb == 0, jj == 1 and mb == 1, [pts[mb].res, mva.res], [om.res])
                dn = den[(2 * hm + jp) % 4]
                for jj in range(2):
                    j = 2 * jp + jj
                    P.ts("dve", dn[:, 4 + jj:5 + jj], om[:, jj * 129 + 128:jj * 129 + 129], 2.0, None, ALU.mult, ALU.bypass, [om.res], [dn.res])
                    P.recip(dn[:, jj:jj + 1], dn[:, 4 + jj:5 + jj], [dn.res], [dn.res])
                    P.stt("dve", ysb[2][:, j, hm * 128:(hm + 1) * 128], om[:, jj * 129:jj * 129 + 128], dn[:, jj:jj + 1],
                          sg[2][:, j, hm * 128:(hm + 1) * 128], ALU.mult, ALU.mult, [om.res, dn.res, sg[2].res], [ysb[2].res])
        for n3, n in ((1, 1), (2, 3)):
            for j in range(4):
                yield
                b = nb()
                bv = b.t[:].bitcast(BF16)
                for c in range(2):
                    P.tr(bv[:, c * 128:(c + 1) * 128], ysb[n3][:, j, c * 128:(c + 1) * 128], ident, [ysb[n3].res] + CR, [b.res])
                P.cp("act" if j % 2 == 0 else "dve", yT[n][:, :, j * 128:(j + 1) * 128],
                     bv[:, 0:256].rearrange("p (c t) -> p c t", c=2), [b.res], [yT[n].res])

    def gen_lru(i):
        pi = i % 2
        uT = uTd[i % 2]
        yield
        w = load_wc(1536, 512, 2)
        for c in range(2):
            yield
            b = proj_fm(w, c * 128, uT)
            if i > 0:
                P.cp("dve", cx[:, c, 0:3], cx[:, c, 512:515], [cx.res], [cx.res])
            P.cp("act", cx[:, c, 3:515], b[:], [b.res], [cx.res])
        for c in range(2):
            yield
            b = proj_fm(w, 256 + c * 128, uT)
            silu2(sgc[:, c, :], sgc.res, b)
        for c in range(2):
            q = c % 2
            xcc, xcbb = xc[q], xcb[q]
            tr_, ti_, tm_, tb_ = lt[0], lt[1], lt[2], lt[3]
            P.ts("dve", xcc[:], cx[:, c, 0:512], cv[:, 3 + c:4 + c], cv[:, 19 + c:20 + c], ALU.mult, ALU.add,
                 [cx.res, cv.res], [xcc.res])
            for k in range(1, 4):
                P.stt("dve", xcc[:], cx[:, c, k:k + 512], cv[:, 3 + 4 * k + c:4 + 4 * k + c], xcc[:], ALU.mult, ALU.add,
                      [cx.res, cv.res, xcc.res], [xcc.res])
            P.cp("act", xcbb[:], xcc[:], [xcc.res], [xcbb.res])
            yield
            ba = nb()
            P.mm(ba[:], lwb[:, 0, c, :], xcbb[:], True, True, [lwb.res, xcbb.res], [ba.res])
            bx = nb()
            P.mm(bx[:], lwb[:, 1, c, :], xcbb[:], True, True, [lwb.res, xcbb.res], [bx.res])
            P.act(tr_[:], ba[:], AF.Tanh, [ba.res, cvh.res], [tr_.res], scale=0.5, bias=cvh[:, c:c + 1])
            P.act(ti_[:], bx[:], AF.Tanh, [bx.res, cvh.res], [ti_.res], scale=0.5, bias=cvh[:, 4 + c:5 + c])
            P.ts("dve", tm_[:], tr_[:], cvh[:, 8 + c:9 + c], cvh[:, 8 + c:9 + c], ALU.mult, ALU.add, [tr_.res, cvh.res], [tm_.res])
            P.act(tr_[:], tm_[:], AF.Exp, [tm_.res], [tr_.res])
            P.act(tm_[:], tm_[:], AF.Exp, [tm_.res], [tm_.res], scale=2.0)
            P.act(tm_[:], tm_[:], AF.Sqrt, [tm_.res, epsb.res], [tm_.res], scale=-1.0, bias=epsb[:, 1:2])
            P.stt("dve", tb_[:], ti_[:], 1.0, xcc[:], ALU.add, ALU.mult, [ti_.res, xcc.res], [tb_.res])
            P.stt("dve", tb_[:], tb_[:], 0.5, tm_[:], ALU.mult, ALU.mult, [tb_.res, tm_.res], [tb_.res])
            hcur = ht[q]

            def scan(e, o=hcur[:], a=tr_[:], bb=tb_[:], init=hcar[:, c:c + 1]):
                return e.tensor_tensor_scan(out=o, data0=a, data1=bb, initial=init, op0=ALU.mult, op1=ALU.add)
            P.op("dve", scan, [tr_.res, tb_.res, hcar.res], [hcur.res])
            P.cp("dve", hcar[:, c:c + 1], hcur[:, 511:512], [hcur.res], [hcar.res])
            P.stt("dve", yT[2][:, c, :], hcur[:], 0.5, sgc[:, c, :], ALU.mult, ALU.mult, [hcur.res, sgc.res], [yT[2].res])
        yield

    def run_interleaved(gens):
        gens = list(gens)
        while gens:
            for gq in list(gens):
                try:
                    next(gq)
                except StopIteration:
                    gens.remove(gq)

    XSTEPS = 8 * 5 + 2 * 5 + 2
    xdone = 0
    gx = None
    for i in range(NT):
        P.dma("sp", uTd[i % 2][:], UT[:, :, i * 512:(i + 1) * 512], [RUT[i]], [uTd[i % 2].res], f"c_cut{i % 2}")
        gl = [gen_swa(i), gen_mem(i), gen_lru(i)]
        if gx is None and i % 2 == 1 and i >= 3 and xdone == (i - 3) // 2:
            gx = stage_X(xdone)
            xdone += 1
            xleft = XSTEPS
        first_half = (i % 2 == 1)
        budget = (XSTEPS // 2) if first_half else 10 ** 9
        while gl:
            for gq in list(gl):
                try:
                    next(gq)
                except StopIteration:
                    gl.remove(gq)
            if gx is not None and budget > 0:
                try:
                    next(gx)
                    budget -= 1
                except StopIteration:
                    gx = None
        if gx is not None:
            while budget > 0:
                try:
                    next(gx)
                    budget -= 1
                except StopIteration:
                    gx = None
                    break
        ysh = YTSH[i // 2].rearrange("(h p c) s -> h p c s", h=2, p=128)
        P.dma("pool", ysh[i % 2], yTall[:], [yTall.res], [RYTS[i]], "c_ytall")
        if i % 2 == 1:
            p_ = i // 2
            P.op("pool", lambda e, p_=p_: e.collective_compute("AllGather", ALU.bypass, RG, [YTSH[p_].opt()], [YTSF[p_].opt()]),
                 [RYTS[2 * p_], RYTS[2 * p_ + 1]], [RYTSF[p_]], chan="c_cc", inc=1)
    if gx is not None:
        for _ in gx:
            pass
    for t_ in range(xdone, NTH):
        run_interleaved([stage_X(t_)])


def _consts():
    s = np.arange(128)[:, None]
    t = np.arange(128)[None, :]
    c = np.zeros((128, 8, 128), np.float32)
    c[:, 0] = np.eye(128)
    c[:, 1] = -((s >= t).astype(np.float32))
    c[:, 2] = -1.0
    c[:, 3] = np.where(s < t, 0.0, NEG)
    c[:, 4] = np.where(s <= t, 0.0, NEG)
    c[:, 5] = np.where(s > t, 0.0, NEG)
    bo = np.zeros((128, 128), np.float32)
    bo[:64, :64] = 1.0 / 64
    bo[64:, 64:] = 1.0 / 64
    c[:, 6] = bo
    c[:, 7] = 1.0 / 128
    return c


def _prep(inputs, L):
    f = lambda a: np.ascontiguousarray(np.asarray(a, dtype=np.float32))
    shared, per = {}, [{}, {}]
    wi = f(inputs["w_in"])
    wm = f(inputs["w_mem_kv"])
    shared["w_branch"] = f(inputs["w_branch"])
    shared["w_out"] = f(inputs["w_out"])
    shared["gain_bc"] = f(np.broadcast_to(np.asarray(inputs["norm_gain"])[:, None, :], (L, 128, 1024)))
    shared["mgain_bc"] = f(np.broadcast_to(np.asarray(inputs["mem_norm_gain"])[:, None, :], (L, 128, 1024)))
    shared["consts"] = _consts()
    merge = np.concatenate([wi[:, :, O_MERGE + n * 1024 + dd * 128:O_MERGE + n * 1024 + (dd + 1) * 128]
                            for dd in range(8) for n in range(4)], axis=2)
    cw = np.asarray(inputs["conv_w"])
    for r in range(2):
        p = per[r]
        o = 256 * r
        p["w_sb"] = np.ascontiguousarray(np.concatenate([wi[:, :, O_BQ + o:O_BQ + o + 256], wi[:, :, O_BK + o:O_BK + o + 256],
                                                         wi[:, :, O_BV + o:O_BV + o + 256]], axis=2))
        kg = wi[:, :, O_AK + 64 * r:O_AK + 64 * (r + 1)]
        vg = wi[:, :, O_AV + 64 * r:O_AV + 64 * (r + 1)]
        p["w_c"] = np.ascontiguousarray(np.concatenate([
            wi[:, :, O_AQ + o:O_AQ + o + 256], kg, kg, vg, vg,
            wi[:, :, O_AG + o:O_AG + o + 256], wi[:, :, O_BG + o:O_BG + o + 256],
            wi[:, :, O_MQ + o:O_MQ + o + 256], wi[:, :, O_MG + o:O_MG + o + 256],
            wi[:, :, O_CX + o:O_CX + o + 256], wi[:, :, O_CG + o:O_CG + o + 256], merge], axis=2))
        oo = 256 * (1 - r)
        p["w_mem_kv"] = np.ascontiguousarray(np.concatenate([wm[:, :, o:o + 256], wm[:, :, oo:oo + 256],
                                                             wm[:, :, 512 + o:512 + o + 256], wm[:, :, 512 + oo:512 + oo + 256]], axis=2))
        colv = np.zeros((L, 128, NCV), np.float32)
        colv[:, :, 0] = np.tile(np.asarray(inputs["swa_q_gain"]), (1, 2))
        colv[:, :, 1] = np.tile(np.asarray(inputs["swa_k_gain"]), (1, 2))
        colv[:, :, 2] = np.asarray(inputs["mem_q_gain"])
        for c in range(2):
            gc = 2 * r + c
            for k in range(4):
                colv[:, :, 3 + 4 * k + c] = cw[:, k, gc * 128:(gc + 1) * 128]
            colv[:, :, 19 + c] = np.asarray(inputs["conv_b"])[:, gc * 128:(gc + 1) * 128]
            colv[:, :, 23 + c] = np.asarray(inputs["lru_b_a"])[:, gc * 128:(gc + 1) * 128]
            colv[:, :, 27 + c] = np.asarray(inputs["lru_b_x"])[:, gc * 128:(gc + 1) * 128]
            colv[:, :, 31 + c] = np.asarray(inputs["lru_lambda"])[:, gc * 128:(gc + 1) * 128]
        p["colv"] = colv
        rowv = np.zeros((L, 128, NRV), np.float32)
        rowv[:, :, 0:4] = np.asarray(inputs["swa_sinks"])[:, None, 4 * r:4 * r + 4]
        rowv[:, :, 8:136] = np.asarray(inputs["mem_k_gain"])[:, None, :]
        p["rowv"] = rowv
        lw = np.zeros((L, 2, 4, 128, 128), np.float32)
        for a, nm in enumerate(("lru_w_a", "lru_w_x")):
            wa = np.asarray(inputs[nm])
            for c in range(2):
                gc = 2 * r + c
                lw[:, a, c, 0:64, 0:64] = wa[:, 2 * gc]
                lw[:, a, c, 64:128, 64:128] = wa[:, 2 * gc + 1]
        p["lruw"] = lw
    return shared, per


_NC_CACHE = {}


def run(inputs, S, L, ncores):
    key = (S, L, ncores)
    if key not in _NC_CACHE:
        _NC_CACHE[key] = build(S, L, ncores)
    nc = _NC_CACHE[key]
    shared, per = _prep(inputs, L)
    x = np.asarray(inputs["x"], dtype=np.float32)
    mem = np.asarray(inputs["mem"], dtype=np.float32)
    in_maps = []
    for c in range(ncores):
        b, r = c // 2, c % 2
        m = dict(shared)
        m.update(per[r])
        m["x"] = np.ascontiguousarray(x[b])
        m["x_own"] = np.ascontiguousarray(x[b].reshape(S // 1024, 2, 512, 1024)[:, r].reshape(S // 2, 1024))
        m["rk"] = np.array([[r, 1 - r]], dtype=np.int32)
        m["mem"] = np.ascontiguousarray(mem[b])
        in_maps.append(m)
    res = run_bass_kernel_spmd(nc, in_maps, core_ids=list(range(ncores)))
    return [r["y"] for r in res.results]


def kernel(**inputs):
    outs = run(inputs, 8192, 2, 8)
    return np.stack([_merge(outs[2 * b], outs[2 * b + 1]) for b in range(4)], axis=0).astype(np.float32)


def _merge(y0, y1):
    n = y0.shape[0] // 512
    return np.stack([y0.reshape(n, 512, -1), y1.reshape(n, 512, -1)], axis=1).reshape(2 * y0.shape[0], -1)
```

```python
import numpy as np
from contextlib import ExitStack
import concourse.bass as bass
import concourse.mybir as mybir
from concourse.bass_utils import run_bass_kernel_spmd

F32 = mybir.dt.float32
BF16 = mybir.dt.bfloat16
AF = mybir.ActivationFunctionType
ALU = mybir.AluOpType
EPS = 1e-6
NEG = -30000.0


class Res:
    __slots__ = ("name", "w", "r")

    def __init__(self, name):
        self.name = name
        self.w = None
        self.r = {}


class TT:
    __slots__ = ("t", "res", "name")

    def __init__(self, t, name):
        self.t = t
        self.name = name
        self.res = Res(name)

    def __getitem__(self, k):
        return self.t[k]


class Prog:
    def __init__(self):
        self.streams = {"pe": [], "act": [], "dve": [], "pool": [], "sp": []}
        self.count = {}
        self.seen = {e: {} for e in self.streams}
        self.chans = []

    def op(self, eng, fn, R=(), W=(), chan=None, inc=None):
        dma = chan is not None
        ch = chan if dma else eng
        if ch not in self.count:
            self.count[ch] = 0
            self.chans.append(ch)
        need = {}

        def add(c, n):
            if need.get(c, 0) < n:
                need[c] = n
        for r in R:
            if r.w is not None:
                add(*r.w)
        for w in W:
            if w.w is not None:
                add(*w.w)
            for c, n in w.r.items():
                add(c, n)
        seen = self.seen[eng]
        waits = []
        for c, n in need.items():
            if c == "pe" and eng == "pe":
                continue
            if seen.get(c, 0) >= n:
                continue
            seen[c] = n
            waits.append((c, n))
        if inc is None:
            inc = 16 if dma else 1
        self.count[ch] += inc
        my = self.count[ch]
        for r in R:
            if r.r.get(ch, 0) < my:
                r.r[ch] = my
        for w in W:
            w.w = (ch, my)
            w.r = {}
        self.streams[eng].append((waits, fn, ch, inc))

    def raw(self, eng, fn):
        self.streams[eng].append(([], fn, None, 0))

    def barrier(self):
        waits = [(c, n) for c, n in self.count.items() if n > 0]
        for eng in self.streams:
            self.streams[eng].append((list(waits), None, None, 0))
            for c, n in waits:
                if self.seen[eng].get(c, 0) < n:
                    self.seen[eng][c] = n

    def mm(self, out, lhsT, rhs, start, stop, R, W):
        self.op("pe", lambda e: e.matmul(out, lhsT, rhs, start=start, stop=stop), R, W)

    def tr(self, out, in_, ident, R, W):
        self.op("pe", lambda e: e.transpose(out, in_, ident), R, W)

    def act(self, out, in_, func, R, W, scale=1.0, bias=None, accum=None):
        kw = {}
        if bias is not None:
            kw["bias"] = bias
        if accum is not None:
            kw["accum_out"] = accum
        self.op("act", lambda e: e.activation(out=out, in_=in_, func=func, scale=scale, **kw), R, W)

    def tt(self, eng, out, a, b, op, R, W):
        self.op(eng, lambda e: e.tensor_tensor(out=out, in0=a, in1=b, op=op), R, W)

    def ts(self, eng, out, a, s1, s2, op0, op1, R, W):
        self.op(eng, lambda e: e.tensor_scalar(out=out, in0=a, scalar1=s1, scalar2=s2, op0=op0, op1=op1), R, W)

    def stt(self, eng, out, a, s, b, op0, op1, R, W):
        self.op(eng, lambda e: e.scalar_tensor_tensor(out=out, in0=a, scalar=s, in1=b, op0=op0, op1=op1), R, W)

    def cp(self, eng, out, in_, R, W):
        if eng == "act":
            self.op("act", lambda e: e.activation(out=out, in_=in_, func=AF.Copy), R, W)
        else:
            self.op(eng, lambda e: e.tensor_copy(out=out, in_=in_), R, W)

    def memset(self, eng, ap, val, W):
        self.op(eng, lambda e: e.memset(ap, val), (), W)

    def recip(self, out, in_, R, W):
        self.op("dve", lambda e: e.reciprocal(out=out, in_=in_), R, W)

    def dma(self, q, out, in_, R, W, chan):
        self.op(q, lambda e: e.dma_start(out=out, in_=in_), R, W, chan=chan)


O_AQ, O_AK, O_AV, O_AG = 0, 512, 640, 768
O_BQ, O_BK, O_BV, O_BG = 1280, 1792, 2304, 2816
O_CX, O_CG, O_MQ, O_MG, O_MERGE = 3328, 3840, 4352, 4864, 5376
IN_W = 9472
NCV = 35
NRV = 8 + 128


def build(S, L, NCORES):
    RG = [[2 * k, 2 * k + 1] for k in range(NCORES // 2)]
    NT = S // 512
    NB = S // 128
    nc = bass.Bass("TRN2", target_bir_lowering=False)
    P = Prog()
    es = ExitStack()

    def din(name, shape, dt=F32):
        return nc.dram_tensor(name, shape, dt, kind="ExternalInput").ap()

    NTH = NT // 2
    x_in = din("x", [S, 1024])
    x_own = din("x_own", [S // 2, 1024])
    rk = nc.dram_tensor("rk", [1, 2], mybir.dt.int32, kind="ExternalInput").ap()
    mem_in = din("mem", [256, 1024])
    w_c = din("w_c", [L, 1024, 6144])
    w_sb = din("w_sb", [L, 1024, 768])
    w_mem = din("w_mem_kv", [L, 1024, 1024])
    w_br = din("w_branch", [L, 4, 512, 1024])
    w_out = din("w_out", [L, 1024, 1024])
    gain_bc = din("gain_bc", [L, 128, 1024])
    mgain_bc = din("mgain_bc", [L, 128, 1024])
    colv = din("colv", [L, 128, NCV])
    rowv = din("rowv", [L, 128, NRV])
    lruw = din("lruw", [L, 2, 4, 128, 128])
    consts = din("consts", [128, 8, 128])
    y_out = nc.dram_tensor("y", [S // 2, 1024], F32, kind="ExternalOutput").ap()

    def dscr(name, shape, dt=BF16):
        return nc.dram_tensor(name, shape, dt).ap()

    WSB = [dscr(f"WSB{l}", [128, 8, 768]) for l in range(L)]
    WC = [dscr(f"WC{l}", [128, 8 * 6144]) for l in range(L)]
    WBR = [dscr(f"WBR{l}", [128, 8, 4, 4, 128]) for l in range(L)]
    WOUT = [dscr(f"WOUT{l}", [128, 8 * 1024]) for l in range(L)]
    WMEM = [dscr(f"WMEM{l}", [128, 8, 1024]) for l in range(L)]
    UT = dscr("UT", [128, 8, S])
    QZ = dscr("QZ", [128, 4, S])
    KT = dscr("KT", [128, 2, S])
    VS = dscr("VS", [NB, 128, 256])
    NPC = max(1, NB // 32)
    NBP = NB // NPC
    YBH = [[dscr(f"YBH{l}_{q}", [NBP * 128, 256]) for q in range(NPC)] for l in range(L)]
    YBF = [[dscr(f"YBF{l}_{q}", [2 * NBP * 128, 256]) for q in range(NPC)] for l in range(L)]
    RYBF = [Res(f"rybf{l}") for l in range(L)]
    X1HL = [[dscr(f"X1H{q}_{t}", [512, 1024], F32) for t in range(NTH)] for q in range(2)]
    X1F = [dscr(f"X1F{t}", [1024, 1024], F32) for t in range(NTH)]
    YTSH = [dscr(f"YTSH{p}", [2 * 128 * 8, 512]) for p in range(NTH)]
    YTSF = [dscr(f"YTSF{p}", [2 * 2 * 128 * 8, 512]) for p in range(NTH)]
    RYTS = [Res(f"ryts{i}") for i in range(NT)]
    RYTSF = [Res(f"rytsf{p}") for p in range(NTH)]
    RWSB = [Res(f"rwsb{l}") for l in range(L)]
    RWC = [Res(f"rwc{l}") for l in range(L)]
    RWBR = [Res(f"rwbr{l}") for l in range(L)]
    RWOUT = [Res(f"rwout{l}") for l in range(L)]
    RWMEM = [Res(f"rwmem{l}") for l in range(L)]
    RUT = [Res(f"rut{i}") for i in range(NT)]
    RQZ = [Res(f"rqz{i}") for i in range(NT)]
    RKT = [Res(f"rkt{i}") for i in range(NT)]
    RVS = [Res(f"rvs{i}") for i in range(NT)]
    RYB = [Res(f"ryb{i}") for i in range(NT)]
    RX1HL = [[Res(f"rx1h{q}_{t}") for t in range(NTH)] for q in range(2)]
    RX1F = [Res(f"rx1f{t}") for t in range(NTH)]
    rkv = {}

    def load_rank(e):
        reg = e.alloc_register("rkreg")
        ins = e.reg_load(reg, rk[0:1, 0:1])
        rkv["r"] = e.snap(reg, min_val=0, max_val=1)
        return ins
    P.raw("sp", load_rank)

    def sb(name, shape, dt=F32):
        return TT(es.enter_context(nc.sbuf_tensor(name, shape, dt)), name)

    ARENA_W = 48 * 1024
    arena = es.enter_context(nc.sbuf_tensor("arena", [128, ARENA_W], F32))
    car = [0]

    def carve(name, shape, dt=F32):
        n = int(np.prod(shape[1:]))
        nw = n if dt == F32 else (n + 1) // 2
        nw = (nw + 7) // 8 * 8
        ap = arena[:, car[0]:car[0] + nw]
        car[0] += nw
        assert car[0] <= ARENA_W, (name, car[0])
        if dt == BF16:
            ap = ap.bitcast(BF16)[:, 0:n]
        else:
            ap = ap[:, 0:n]
        if len(shape) == 3:
            ap = ap.rearrange("p (a b) -> p a b", a=shape[1])
        elif len(shape) == 4:
            ap = ap.rearrange("p (a b c) -> p a b c", a=shape[1], b=shape[2])
        return TT(ap, name)

    def phase():
        P.barrier()
        car[0] = 0

    psall = es.enter_context(nc.psum_tensor("psall", [128, 8, 512], F32))
    banks = [TT(psall[:, i, :], f"bank{i}") for i in range(8)]
    bank_ctr = [0]

    def nb():
        b = banks[bank_ctr[0] % 8]
        bank_ctr[0] += 1
        return b

    cst_f = carve("cst_f", [128, 8, 128])
    cst = sb("cst", [128, 8, 128], BF16)
    P.dma("sp", cst_f[:], consts, [], [cst_f.res], "c_cst")
    P.cp("dve", cst[:], cst_f[:], [cst_f.res], [cst.res])
    ident = cst[:, 0, :]
    negtri = cst[:, 1, :]
    negones = cst[:, 2, :]
    m_sb = cst[:, 3, :]
    m_cur = cst[:, 4, :]
    m_prev = cst[:, 5, :]
    bones64 = cst[:, 6, :]
    ones128 = cst[:, 7, :]
    mask4 = sb("mask4", [128, 2, 4, 128], BF16)
    for q in range(4):
        P.cp("dve", mask4[:, 0, q, :], cst_f[:, 5, :], [cst_f.res], [mask4.res])
        P.cp("dve", mask4[:, 1, q, :], cst_f[:, 4, :], [cst_f.res], [mask4.res])
    CR = [cst.res]

    def wcast(dst, src_rows_cols, res, chan):
        P.dma("pool", dst, src_rows_cols.rearrange("(k p) c -> p k c", p=128), [], [res], chan)

    for l in range(L):
        wcast(WSB[l], w_sb[l], RWSB[l], f"c_wsb{l}")
        for c0, n_ in [(0, 512), (512, 256), (768, 256), (1024, 512), (1536, 512)] + [(2048 + 512 * d_, 512) for d_ in range(8)]:
            wcast(WC[l][:, 8 * c0:8 * (c0 + n_)].rearrange("p (k c) -> p k c", k=8), w_c[l][:, c0:c0 + n_], RWC[l], f"c_wc{l}")
        for n in range(4):
            for d in range(8):
                P.dma("pool", WBR[l][:, d, n, :, :],
                      w_br[l, n, :, d * 128:(d + 1) * 128].rearrange("(k p) c -> p k c", p=128),
                      [], [RWBR[l]], f"c_wbr{l}")
        for c0 in (0, 512):
            wcast(WOUT[l][:, 8 * c0:8 * (c0 + 512)].rearrange("p (k c) -> p k c", k=8), w_out[l][:, c0:c0 + 512], RWOUT[l], f"c_wout{l}")
        wcast(WMEM[l], w_mem[l], RWMEM[l], f"c_wmem{l}")

    cv = sb("cv", [128, NCV])
    rv = sb("rv", [128, NRV])
    cvx = sb("cvx", [128, 8])
    cvh = sb("cvh", [128, 12])
    exps = sb("exps", [128, 8])
    lwb = sb("lwb", [128, 2, 4, 128], BF16)
    SS = [sb(f"ss{i}", [128, 4]) for i in range(4)]
    SSM = [sb(f"ssm{i}", [128, 4]) for i in range(4)]
    mkT = sb("mkT", [128, 4, 256], BF16)
    mva = sb("mva", [128, 2, 4, 129], BF16)
    P.memset("pool", mva[:], 1.0, [mva.res])
    epsb = sb("epsb", [128, 2])
    P.memset("dve", epsb[:, 0:1], EPS, [epsb.res])
    P.memset("dve", epsb[:, 1:2], 1.0, [epsb.res])
    junkh = [None]

    def rms_rows(xs, gain_tile, out_bf, sst):
        junk = junkh[0]
        P.act(junk[:], xs[:], AF.Square, [xs.res], [junk.res, sst.res], accum=sst[:, 0:1])
        P.act(sst[:, 1:2], sst[:, 0:1], AF.Sqrt, [sst.res], [sst.res], scale=1.0 / 1024, bias=epsb[:, 0:1])
        P.recip(sst[:, 2:3], sst[:, 1:2], [sst.res], [sst.res])
        P.stt("dve", out_bf[:], xs[:], sst[:, 2:3], gain_tile[:], ALU.mult, ALU.mult,
              [xs.res, sst.res, gain_tile.res], [out_bf.res])

    def transpose_to(src_bf, dst_ap_fn, nchunk, evac_eng):
        b = nb()
        bv = b.t[:].bitcast(BF16)
        for c in range(nchunk):
            P.tr(bv[:, c * 128:(c + 1) * 128], src_bf[:, c * 128:(c + 1) * 128], ident, [src_bf.res] + CR, [b.res])
        return b, bv

    for l in range(L):
        phase()
        gbc = carve("gbc", [128, 1024])
        lw = carve("lw", [128, 2, 4, 128])
        wbig = carve("wbig", [128, 8, 1024], BF16)
        XS = [carve(f"xs{i}", [128, 1024]) for i in range(4)]
        junk = carve("junk", [128, 1024], BF16)
        junkh[0] = junk
        ubf = [carve(f"ubf{i}", [128, 1024], BF16) for i in range(2)]
        ubf4 = ubf + [carve(f"ubf{i}", [128, 1024], BF16) for i in range(2, 4)]
        UTB = [carve(f"utb{i}", [128, 8, 512], BF16) for i in range(2)]
        QZT = [carve(f"qzt{i}", [128, 4, 512], BF16) for i in range(2)]
        KTT = [carve(f"ktt{i}", [128, 2, 512], BF16) for i in range(2)]
        VTT = [carve(f"vtt{i}", [128, 4, 256], BF16) for i in range(2)]
        mnT = carve("mnT", [128, 8, 256], BF16)
        mkn = carve("mkn", [128, 512], BF16)
        for q in QZT:
            P.memset("pool", q[:], 0.0, [q.res])
        P.dma("sp", gbc[:], gain_bc[l], [], [gbc.res], "c_gbc")
        P.dma("sp", cv[:], colv[l], [], [cv.res], "c_cv")
        P.dma("sp", rv[:], rowv[l], [], [rv.res], "c_rv")
        P.dma("sp", lw[:], lruw[l].rearrange("a c p j -> p a c j"), [], [lw.res], "c_lw")
        P.cp("dve", lwb[:], lw[:], [lw.res], [lwb.res])
        P.ts("dve", cvx[:, 0:1], cv[:, 0:1], 0.125, None, ALU.mult, ALU.bypass, [cv.res], [cvx.res])
        P.ts("dve", cvx[:, 1:2], cv[:, 2:3], float(128 ** -0.5), None, ALU.mult, ALU.bypass, [cv.res], [cvx.res])
        P.act(cvx[:, 2:6], cv[:, 31:35], AF.Exp, [cv.res], [cvx.res], scale=-1.0)
        P.act(cvx[:, 2:6], cvx[:, 2:6], AF.Ln, [cvx.res], [cvx.res], bias=epsb[:, 1:2])
        P.ts("dve", cvx[:, 2:6], cvx[:, 2:6], -8.0, None, ALU.mult, ALU.bypass, [cvx.res], [cvx.res])
        P.act(exps[:], rv[:, 0:8], AF.Exp, [rv.res], [exps.res])
        P.ts("dve", cvh[:, 0:8], cv[:, 23:31], 0.5, None, ALU.mult, ALU.bypass, [cv.res], [cvh.res])
        P.ts("dve", cvh[:, 8:12], cvx[:, 2:6], 0.5, None, ALU.mult, ALU.bypass, [cvx.res], [cvh.res])

        wmemb = carve("wmemb", [128, 8, 1024], BF16)
        MXS = [carve(f"mxs{i}", [128, 1024]) for i in range(3)]
        mubf = [carve(f"mubf{i}", [128, 1024], BF16) for i in range(2)]

        def gen_M():
            P.dma("sp", wmemb[:], WMEM[l], [RWMEM[l]], [wmemb.res], "c_wmemb")
            mg = MXS[2]
            P.dma("sp", mg[:], mgain_bc[l], [], [mg.res], "c_mxs2")
            for r in range(2):
                xs = MXS[r]
                P.dma("sp", xs[:], mem_in[r * 128:(r + 1) * 128, :], [], [xs.res], f"c_mxs{r}")
            yield
            for r in range(2):
                xs = MXS[r]
                rms_rows(xs, mg, mubf[r], SSM[r])
                yield
                b, bv = transpose_to(mubf[r], None, 8, "act")
                P.cp("act", mnT[:, :, r * 128:(r + 1) * 128], bv.rearrange("p (c t) -> p c t", c=8), [b.res], [mnT.res])
                yield
            for r in range(2):
                b = nb()
                for k in range(8):
                    P.mm(b[:], mnT[:, k, r * 128:(r + 1) * 128], wmemb[:, k, 0:512], k == 0, k == 7, [mnT.res, wmemb.res], [b.res])
                sst = SSM[2 + r]
                for hm in range(4):
                    P.act(junk[:, 0:128], b[:, hm * 128:(hm + 1) * 128], AF.Square, [b.res], [junk.res, sst.res],
                          accum=sst[:, hm:hm + 1])
                P.act(sst[:], sst[:], AF.Sqrt, [sst.res], [sst.res], scale=1.0 / 128, bias=epsb[:, 0:1])
                P.recip(sst[:], sst[:], [sst.res], [sst.res])
                for hm in range(4):
                    P.stt("dve", mkn[:, hm * 128:(hm + 1) * 128], b[:, hm * 128:(hm + 1) * 128], sst[:, hm:hm + 1],
                          rv[:, 8:136], ALU.mult, ALU.mult, [b.res, sst.res, rv.res], [mkn.res])
                yield
                b2, bv2 = transpose_to(mkn, None, 4, "act")
                P.cp("act", mkT[:, :, r * 128:(r + 1) * 128], bv2[:, 0:512].rearrange("p (c t) -> p c t", c=4), [b2.res], [mkT.res])
                yield
                b = nb()
                for k in range(8):
                    P.mm(b[:], mnT[:, k, r * 128:(r + 1) * 128], wmemb[:, k, 512:1024], k == 0, k == 7, [mnT.res, wmemb.res], [b.res])
                P.cp("act", mva[:, r, :, 0:128], b[:].rearrange("p (h d) -> p h d", h=4), [b.res], [mva.res])
                yield

        P.dma("sp", wbig[:, :, 0:768], WSB[l], [RWSB[l]], [wbig.res], "c_wbig")

        def gen_norm(i, j):
            uT = UTB[i % 2]
            row0 = i * 512 + j * 128
            xs = XS[j]
            if l == 0:
                P.dma("sp", xs[:], x_in[row0:row0 + 128, :], [], [xs.res], f"c_xs{j}")
            else:
                xr0 = (i % 2) * 512 + j * 128
                P.dma("sp", xs[:], X1F[i // 2][xr0:xr0 + 128, :], [RX1F[i // 2]], [xs.res], f"c_xs{j}")
            yield
            u = ubf4[j]
            rms_rows(xs, gbc, u, SS[j])
            yield
            b, bv = transpose_to(u, None, 8, "act")
            P.cp("act" if j % 2 == 0 else "dve", uT[:, :, j * 128:(j + 1) * 128],
                 bv.rearrange("p (c t) -> p c t", c=8), [b.res], [uT.res])

        def gen_sb(i):
            uT = UTB[i % 2]
            qz, kt, vt = QZT[i % 2], KTT[i % 2], VTT[i % 2]
            for c in range(2):
                yield
                b = nb()
                for k in range(8):
                    P.mm(b[:], wbig[:, k, c * 128:(c + 1) * 128], uT[:, k, :], k == 0, k == 7, [wbig.res, uT.res], [b.res])
                P.act(qz[0:64, 2 * c, :], b[0:64, :], AF.Copy, [b.res], [qz.res], scale=0.125)
                P.ts("dve", qz[64:128, 2 * c + 1, :], b[64:128, :], 0.125, None, ALU.mult, ALU.bypass, [b.res], [qz.res])
            for c in range(2):
                yield
                b = nb()
                for k in range(8):
                    P.mm(b[:], wbig[:, k, 256 + c * 128:256 + (c + 1) * 128], uT[:, k, :], k == 0, k == 7,
                         [wbig.res, uT.res], [b.res])
                P.cp("act" if c % 2 == 0 else "dve", kt[:, c, :], b[:], [b.res], [kt.res])
            for j in range(4):
                yield
                b = nb()
                for k in range(8):
                    P.mm(b[:, 0:256], uT[:, k, j * 128:(j + 1) * 128], wbig[:, k, 512:768], k == 0, k == 7,
                         [wbig.res, uT.res], [b.res])
                P.cp("act" if j % 2 == 0 else "dve", vt[:, j, :], b[:, 0:256], [b.res], [vt.res])
            P.dma("pool", QZ[:, :, i * 512:(i + 1) * 512], qz[:], [qz.res], [RQZ[i]], f"c_qzt{i % 2}")
            P.dma("pool", KT[:, :, i * 512:(i + 1) * 512], kt[:], [kt.res], [RKT[i]], f"c_ktt{i % 2}")
            P.dma("pool", VS[4 * i:4 * i + 4].rearrange("j p c -> p j c"), vt[:], [vt.res], [RVS[i]], f"c_vtt{i % 2}")

        def run_il(gens):
            gens = list(gens)
            while gens:
                for gq in list(gens):
                    try:
                        next(gq)
                    except StopIteration:
                        gens.remove(gq)

        gm = gen_M()
        for i in range(NT):
            gl = [gen_norm(i, j) for j in range(4)]
            if i > 0:
                gl.append(gen_sb(i - 1))
            if i == 0:
                gl.append(gm)
            run_il(gl)
            P.dma("pool", UT[:, :, i * 512:(i + 1) * 512], UTB[i % 2][:], [UTB[i % 2].res], [RUT[i]], f"c_utb{i % 2}")
        run_il([gen_sb(NT - 1)])

        phase()
        ybh_views = [t.rearrange("(b p) c -> b p c", p=128) for t in YBH[l]]
        pass_B(P, carve, l, S, NT, NB, banks, psall, QZ, KT, VS, ybh_views, NBP, RQZ, RKT, RVS, RYB, cst, CR)

        phase()
        pass_C(P, carve, l, L, S, NT, locals())

    fin = [(c, n) for c, n in P.count.items() if c.startswith("c_")]
    P.streams["pool"].append((fin, None, None, 0))

    sems = {c: es.enter_context(nc.semaphore("s_" + c)) for c in P.chans}
    block = es.enter_context(nc.Block())

    def emit(stream):
        def f(eng):
            for waits, fn, ch, inc in stream:
                for c, n in waits:
                    eng.wait_ge(sems[c], n)
                if fn is not None:
                    r = fn(eng)
                    if ch is not None:
                        r.then_inc(sems[ch], inc)
        return f

    block.tensor(emit(P.streams["pe"]))
    block.scalar(emit(P.streams["act"]))
    block.vector(emit(P.streams["dve"]))
    block.gpsimd(emit(P.streams["pool"]))
    block.sync(emit(P.streams["sp"]))
    es.close()
    return nc


def pass_B(P, carve, l, S, NT, NB, banks, psall, QZ, KT, VS, YBV, NBP, RQZ, RKT, RVS, RYB, cst, CR):
    KB = 32
    NS = KB
    NR = KB + 8
    qzp = carve("b_qzp", [128, 2, S], BF16)
    ktp = carve("b_ktp", [128, S], BF16)
    vp = carve("b_vp", [128, NB, 128], BF16)
    LBt = carve("b_lall", [128, NS, 512], BF16)
    LB = [TT(LBt[:, i, :], f"b_l{i}") for i in range(NS)]
    RBT = [carve(f"b_r{i}", [128, 512], BF16) for i in range(NR)]
    ABt = carve("b_aall", [128, NS, 512], BF16)
    AB = [TT(ABt[:, i, :], f"b_a{i}") for i in range(NS)]
    YT = [carve(f"b_y{i}", [128, 4, 128], BF16) for i in range(2)]
    ident, negtri, negones, m_sb = cst[:, 0, :], cst[:, 1, :], cst[:, 2, :], cst[:, 3, :]
    ZP = [(0, 1), (2, 3)]
    ACC = [[banks[4], banks[5]], [banks[6], banks[7]]]
    NCH = min(4, NT)
    CT = S // NCH
    qres = [Res(f"b_qres{c}") for c in range(NCH)]
    kres = [Res(f"b_kres{c}") for c in range(NCH)]
    vres = [Res(f"b_vres{c}") for c in range(NCH)]
    for hp in range(2):
        for c in range(NCH):
            t0, t1 = c * CT, (c + 1) * CT
            P.dma("sp", qzp[:, :, t0:t1], QZ[:, 2 * hp:2 * hp + 2, t0:t1], RQZ, [qres[c]], f"c_qzp{c}")
            P.dma("sp", ktp[:, t0:t1], KT[:, hp, t0:t1], RKT, [kres[c]], f"c_ktp{c}")
            b0, b1 = t0 // 128, t1 // 128
            P.dma("sp", vp[:, b0:b1, :], VS[b0:b1, :, hp * 128:(hp + 1) * 128].rearrange("b p c -> p b c"),
                  RVS, [vres[c]], f"c_vp{c}")
        items = []
        for i in range(NT):
            nblk = 4 * i + 4
            for n, kb in enumerate(reversed(range(nblk))):
                for s in range(2):
                    jd = kb - 4 * i if kb >= 4 * i else -1
                    items.append(dict(s=s, i=i, kb=kb, n=n, jd=jd, c0=(128 * jd if jd >= 0 else 0),
                                      first=(n == 0), last=(n == nblk - 1)))
        NI = len(items)

        def zmm(w, zb):
            it = items[w]
            c0, s, i, kb = it["c0"], it["s"], it["i"], it["kb"]
            diag = it["jd"] >= 0
            P.mm(zb[:, c0:512], ktp[:, kb * 128:(kb + 1) * 128], qzp[:, s, i * 512 + c0:(i + 1) * 512],
                 True, False, [kres[(kb * 128) // CT], qres[(i * 512) // CT]], [zb.res])
            if diag:
                P.mm(zb[:, c0:c0 + 128], ident, m_sb, False, False, CR, [zb.res])

        def rupd(w):
            it = items[w]
            lt = LB[w % NS]
            c0 = it["c0"]
            if not it["last"]:
                rn = RBT[(w + 2) % NR]
                if it["first"]:
                    if c0 > 0:
                        P.memset("pool", rn[:, 0:c0], 0.0, [rn.res])
                    P.cp("pool", rn[:, c0:512], lt[:, c0:512], [lt.res], [rn.res])
                else:
                    ro = RBT[w % NR]
                    if c0 > 0:
                        P.cp("pool", rn[:, 0:c0], ro[:, 0:c0], [ro.res], [rn.res])
                    P.tt("dve", rn[:, c0:512], ro[:, c0:512], lt[:, c0:512], ALU.add, [ro.res, lt.res], [rn.res])

        def P1(w):
            it = items[w]
            c0 = it["c0"]
            bp = ZP[(w // 2) % 2]
            zmm(w, banks[bp[0]])
            zmm(w + 1, banks[bp[1]])
            s0 = w % NS
            P.act(LBt[:, s0:s0 + 2, c0:512], psall[:, bp[0]:bp[0] + 2, c0:512], AF.Softplus,
                  [banks[bp[0]].res, banks[bp[1]].res], [LB[s0].res, LB[s0 + 1].res])
            rupd(w)
            rupd(w + 1)

        def P2a(w):
            bp = ZP[(w // 2) % 2]
            for q in range(2):
                it = items[w + q]
                zb, lt = banks[bp[q]], LB[(w + q) % NS]
                c0 = it["c0"]
                zmm(w + q, zb)
                P.mm(zb[:, c0:512], negtri, lt[:, c0:512], False, it["first"], [lt.res] + CR, [zb.res])
                if not it["first"]:
                    ro = RBT[(w + q) % NR]
                    P.mm(zb[:, c0:512], negones, ro[:, c0:512], False, True, [ro.res] + CR, [zb.res])

        def P2b(w):
            it = items[w]
            c0 = it["c0"]
            bp = ZP[(w // 2) % 2]
            s0 = w % NS
            P.act(ABt[:, s0:s0 + 2, c0:512], psall[:, bp[0]:bp[0] + 2, c0:512], AF.Exp,
                  [banks[bp[0]].res, banks[bp[1]].res], [AB[s0].res, AB[s0 + 1].res])

        def P2c(w):
            it = items[w]
            a = AB[w % NS]
            s, i, kb = it["s"], it["i"], it["kb"]
            acc = ACC[s][i % 2]
            j0 = max(it["jd"], 0)
            for j in range(j0, 4):
                P.mm(acc[:, j * 64:(j + 1) * 64], a[:, j * 128:(j + 1) * 128], vp[:, kb, s * 64:(s + 1) * 64],
                     it["first"] and j == j0, it["last"] and j == 3, [a.res, vres[(kb * 128) // CT]], [acc.res])
            if it["last"]:
                yt = YT[i % 2]
                P.cp("dve", yt[:, :, s * 64:(s + 1) * 64], acc[:, 0:256].rearrange("p (j d) -> p j d", j=4),
                     [acc.res], [yt.res])
                if s == 1:
                    ybq, ybo = YBV[(4 * i) // NBP], (4 * i) % NBP
                    P.dma("pool", ybq[ybo:ybo + 4, :, hp * 128:(hp + 1) * 128].rearrange("j p c -> p j c"), yt[:],
                          [yt.res], [RYB[i]], f"c_byt{i % 2}")

        nxt = 0
        for b0 in range(0, NI, KB):
            b1 = min(NI, b0 + KB)
            for w in range(b0, b1, 2):
                P1(w)
                for _ in range(2):
                    if nxt < b0:
                        P2c(nxt)
                        nxt += 1
            while nxt < b0:
                P2c(nxt)
                nxt += 1
            P2a(b0)
            for w in range(b0, b1, 2):
                if w + 2 < b1:
                    P2a(w + 2)
                P2b(w)
        while nxt < NI:
            P2c(nxt)
            nxt += 1


def pass_C(P, carve, l, L, S, NT, env):
    g = env
    banks = g["banks"]
    nbc = [0]

    def nb():
        b = banks[nbc[0] % 6]
        nbc[0] += 1
        return b
    cst, CR = g["cst"], g["CR"]
    ident, bones64, ones128 = cst[:, 0, :], cst[:, 6, :], cst[:, 7, :]
    mask4 = g["mask4"]
    cv, rv, cvx, exps, lwb, epsb, cvh = g["cv"], g["rv"], g["cvx"], g["exps"], g["lwb"], g["epsb"], g["cvh"]
    mkT, mva = g["mkT"], g["mva"]
    UT, WC, WBR, WOUT = g["UT"], g["WC"], g["WBR"], g["WOUT"]
    RUT, RYB, RWC, RWBR, RWOUT = g["RUT"], g["RYB"], g["RWC"], g["RWBR"], g["RWOUT"]
    NTH, rkv, YTSH, YTSF, RYTS, RYTSF = g["NTH"], g["rkv"], g["YTSH"], g["YTSF"], g["RYTS"], g["RYTSF"]
    YBHv, NBP = g["ybh_views"], g["NBP"]
    X1F, RX1F, x_own, y_out, RG = g["X1F"], g["RX1F"], g["x_own"], g["y_out"], g["RG"]
    X1H, RX1H = g["X1HL"][(l + 1) % 2], g["RX1HL"][(l + 1) % 2]
    XN, RXN = g["X1HL"][l % 2], g["RX1HL"][l % 2]
    _PC = {"ctr": 0, "mctr": 0, "oactr": 0}
    uTd = [carve(f"c_ut{i}", [128, 8, 512], BF16) for i in range(2)]
    WY = carve("c_wy", [128, 8, 2048], BF16)
    uTx = carve("c_utx", [128, 8, 512], BF16)
    yTx = carve("c_ytx", [128, 4, 4, 512], BF16)
    XSc = [carve(f"c_xs{i}", [128, 512]) for i in range(2)]
    WCS = [None, None, None] + [carve(f"c_wcs{i}", [128, 8, 512], BF16) for i in range(3, 5)]
    WBS = [carve(f"c_wbs{i}", [128, 4, 4, 128], BF16) for i in range(2)]
    qa = [carve(f"c_qa{h}", [128, 2, 512], BF16) for h in range(2)]
    kn = [[carve(f"c_kn{a}{i}", [128, 512], BF16) for i in range(2)] for a in range(1)]
    va = [carve(f"c_va{i}", [128, 4, 2, 65], BF16) for i in range(2)]
    sq = [carve(f"c_sq{i}", [128, 512], BF16) for i in range(2)]
    rs = [carve(f"c_rs{i}", [128, 512]) for i in range(2)]
    sg = [carve(f"c_sg{n}", [128, 4, 256], BF16) for n in range(3)]
    ysb = sg
    ybt = carve("c_ybt", [128, 4, 256], BF16)
    yTall = carve("c_yTall", [128, 8, 512], BF16)

    class _V:
        def __init__(self, t, n):
            self.t, self.n, self.res = t, n, t.res

        def __getitem__(self, k):
            if isinstance(k[1], slice):
                s0 = 2 * self.n + (k[1].start or 0)
                s1 = 2 * self.n + (k[1].stop if k[1].stop is not None else 2)
                return self.t[(k[0], slice(s0, s1)) + tuple(k[2:])]
            return self.t[(k[0], 2 * self.n + k[1]) + tuple(k[2:])]
    yT = [_V(yTall, n) for n in range(4)]
    psb_s = [carve(f"c_ps{i}", [128, 512], BF16) for i in range(2)]
    psb_m = [carve(f"c_pm{i}", [128, 512], BF16) for i in range(3)]
    den = [carve(f"c_den{i}", [128, 8]) for i in range(4)]
    qmn = carve("c_qmn", [128, 2, 512], BF16)
    cx = carve("c_cx", [128, 2, 515])
    ht = [carve("c_ht0", [128, 512])] * 2
    hcar = carve("c_hcar", [128, 8])
    xc = [carve("c_xc0", [128, 512])] * 2
    xcb = [carve("c_xcb0", [128, 512], BF16)] * 2
    lt = [carve(f"c_lt{q}", [128, 512]) for q in range(4)]
    sgc = carve("c_sgc", [128, 2, 512], BF16)
    gs = [carve(f"c_gs{i}", [128, 512]) for i in range(2)]
    tq = [carve(f"c_tq{i}", [128, 512]) for i in range(2)]
    tqc = [0]

    def silu2(out_ap, out_res, b_ap, b_res=None):
        if b_res is None:
            b_ap, b_res = b_ap[:], b_ap.res
        n = b_ap.shape[-1]
        t = tq[tqc[0] % 2]
        tqc[0] += 1
        P.act(t[:, 0:n], b_ap, AF.Tanh, [b_res], [t.res], scale=0.5)
        P.stt("dve", out_ap, t[:, 0:n], 1.0, b_ap, ALU.add, ALU.mult, [t.res, b_res], [out_res])
    mx = [carve(f"c_mx{i}", [128, 512]) for i in range(3)]
    mixT = carve("c_mixT", [128, 8, 512], BF16)
    xo = [carve(f"c_xo{i}", [128, 512]) for i in range(2)]
    for q in qa:
        P.memset("pool", q[:], 0.0, [q.res])
    for q in va:
        P.memset("pool", q[:], 1.0, [q.res])
    P.memset("pool", cx[:], 0.0, [cx.res])
    P.memset("pool", hcar[:], 0.0, [hcar.res])

    class _W:
        def __init__(self, t, off, res):
            self.t, self.off, self.res = t, off, res

        def __getitem__(self, k):
            s = k[2]
            return self.t[(k[0], k[1], slice(self.off + s.start, self.off + s.stop))]

    wyres = {}
    for c0_, n_ in [(0, 512), (768, 256), (1536, 512), (512, 256), (1024, 512)]:
        wyres[c0_] = Res(f"c_wyres{c0_}")
        P.dma("sp", WY[:, :, c0_:c0_ + n_], WC[l][:, 8 * c0_:8 * (c0_ + n_)].rearrange("p (k c) -> p k c", k=8),
              [RWC[l]], [wyres[c0_]], f"c_wy{c0_}")

    def load_wc(col0, ncols, slot, src=None, res=None):
        if slot < 3:
            return _W(WY, col0, wyres[col0])
        w = WCS[slot]
        if src is None:
            src, res = WC[l], RWC[l]
        P.dma("sp", w[:, :, 0:ncols], src[:, 8 * col0:8 * (col0 + ncols)].rearrange("p (k c) -> p k c", k=8),
              [res], [w.res], "c_" + w.name)
        return w

    def proj_fm(w, col0, uT):
        b = nb()
        for k in range(8):
            P.mm(b[:], w[:, k, col0:col0 + 128], uT[:, k, :], k == 0, k == 7, [w.res, uT.res], [b.res])
        return b

    def proj_tm(w, col0, ncol, uT, j):
        b = nb()
        for k in range(8):
            P.mm(b[:, 0:ncol], uT[:, k, j * 128:(j + 1) * 128], w[:, k, col0:col0 + ncol], k == 0, k == 7,
                 [w.res, uT.res], [b.res])
        return b

    tog = [0]

    def qknorm(b, onesmat, nfeat, writes):
        q = tog[0] % 2
        tog[0] += 1
        P.act(sq[q][:], b[:], AF.Square, [b.res], [sq[q].res])
        b2 = nb()
        P.mm(b2[:], onesmat, sq[q][:], True, True, [sq[q].res] + CR, [b2.res])
        P.act(rs[q][:], b2[:], AF.Sqrt, [b2.res], [rs[q].res], bias=epsb[:, 0:1])
        P.recip(rs[q][:], rs[q][:], [rs[q].res], [rs[q].res])
        return rs[q]

    UTv = UT.rearrange("p c (t h s) -> h t p c s", h=2, s=512)

    def stage_X(t):
        P.op("sp", lambda e: e.dma_start(out=uTx[:], in_=UTv[rkv["r"], t]),
             RUT, [uTx.res], chan="c_utx")
        ysf = YTSF[t].rearrange("(r h p n c) s -> r h p n c s", r=2, h=2, p=128, n=4)
        for sr in range(2):
            P.op("sp", lambda e, sr=sr: e.dma_start(out=yTx[:, :, 2 * sr:2 * sr + 2, :], in_=ysf[sr, rkv["r"]]),
                 [RYTSF[t]], [yTx.res], chan="c_ytx")
        xsrcs = [(2048 + d_ * 512, None, None) for d_ in range(8)] + [(ch_ * 512, WOUT[l], RWOUT[l]) for ch_ in range(2)]

        def xload(q):
            c0, s_, r_ = xsrcs[q]
            return load_wc(c0, 512, 3 + q % 2, s_, r_)
        wnext = xload(0)
        for d in range(8):
            yield
            w = wnext
            wnext = xload(d + 1)
            wb = WBS[d % 2]
            P.dma("sp", wb[:], WBR[l][:, d], [RWBR[l]], [wb.res], "c_" + wb.name)
            for n in range(4):
                yield
                bg = proj_fm(w, n * 128, uTx)
                gn = gs[n % 2]
                P.act(gn[:], bg[:], AF.Tanh, [bg.res], [gn.res], scale=0.5)
                bu = nb()
                for kc in range(4):
                    P.mm(bu[:], wb[:, n, kc, :], yTx[:, n, kc, :], kc == 0, kc == 3, [wb.res, yTx.res], [bu.res])
                if n == 0:
                    P.stt("dve", mx[0][:], gn[:], 1.0, bu[:], ALU.add, ALU.mult, [gn.res, bu.res], [mx[0].res])
                else:
                    tmp = mx[1 + n % 2]
                    P.stt("dve", tmp[:], gn[:], 1.0, bu[:], ALU.add, ALU.mult, [gn.res, bu.res], [tmp.res])
                    if n < 3:
                        P.tt("dve", mx[0][:], mx[0][:], tmp[:], ALU.add, [mx[0].res, tmp.res], [mx[0].res])
                    else:
                        P.tt("dve", mixT[:, d, :], mx[0][:], tmp[:], ALU.add, [mx[0].res, tmp.res], [mixT.res])
        for ch in range(2):
            yield
            w = wnext
            if ch == 0:
                wnext = xload(9)
            for j in range(4):
                row0 = t * 512 + j * 128
                xs = XSc[j % 2]
                if l == 0:
                    P.dma("sp", xs[:], x_own[row0:row0 + 128, ch * 512:(ch + 1) * 512], [], [xs.res], "c_" + xs.name)
                else:
                    P.dma("sp", xs[:], X1H[t][j * 128:(j + 1) * 128, ch * 512:(ch + 1) * 512], [RX1H[t]], [xs.res], "c_" + xs.name)
                yield
                b = nb()
                for k in range(8):
                    P.mm(b[:], mixT[:, k, j * 128:(j + 1) * 128], w[:, k, :], k == 0, k == 7, [mixT.res, w.res], [b.res])
                o = xo[j % 2]
                P.stt("dve", o[:], b[:], 0.5, xs[:], ALU.mult, ALU.add, [b.res, xs.res], [o.res])
                if l == L - 1:
                    P.dma("pool", y_out[row0:row0 + 128, ch * 512:(ch + 1) * 512], o[:], [o.res], [], "c_" + o.name)
                else:
                    P.dma("pool", XN[t][j * 128:(j + 1) * 128, ch * 512:(ch + 1) * 512], o[:], [o.res], [RXN[t]], "c_" + o.name)
        if l < L - 1:
            P.op("pool", lambda e: e.collective_compute("AllGather", ALU.bypass, RG, [XN[t].opt()], [X1F[t].opt()]),
                 [RXN[t]], [RX1F[t]], chan="c_cc", inc=1)

    def gen_swa(i):
        pi = i % 2
        uT = uTd[i % 2]
        yield
        w = load_wc(0, 512, 0)
        for c in range(2):
            yield
            b = proj_fm(w, c * 128, uT)
            r = qknorm(b, bones64, 64, None)
            P.stt("dve", qa[0][0:64, c, :], b[0:64, :], cvx[0:64, 0:1], r[0:64, :], ALU.mult, ALU.mult,
                  [b.res, r.res, cvx.res], [qa[0].res])
            P.stt("dve", qa[1][64:128, c, :], b[64:128, :], cvx[64:128, 0:1], r[64:128, :], ALU.mult, ALU.mult,
                  [b.res, r.res, cvx.res], [qa[1].res])
        for a in range(1):
            yield
            b = proj_fm(w, 256, uT)
            r = qknorm(b, bones64, 64, None)
            P.stt("dve", kn[a][pi][:], b[:], cv[:, 1:2], r[:], ALU.mult, ALU.mult, [b.res, r.res, cv.res], [kn[a][pi].res])
        for j in range(4):
            yield
            b = proj_tm(w, 384, 64, uT, j)
            P.cp("act", va[pi][:, j, 0, 0:64], b[:, 0:64], [b.res], [va[pi].res])
        yield
        w = load_wc(512, 256, 0)
        for j in range(4):
            yield
            b = proj_tm(w, 0, 256, uT, j)
            silu2(sg[0][:, j, 0:256], sg[0].res, b[:, 0:256], b.res)
        for j in range(4):
            nblk = 4 * i + j
            for gq in range(1):
                yield
                oa = banks[6 + _PC["oactr"] % 2]
                _PC["oactr"] += 1
                kbs = ([0] if nblk > 0 else []) + [1]
                for kk, which in enumerate(kbs):
                    if which == 1:
                        ksrc = lambda a: kn[a][pi][:, j * 128:(j + 1) * 128]
                        kres = [kn[0][pi].res]
                        vsrc, vres = va[pi][:, j, gq, :], va[pi].res
                    else:
                        if j > 0:
                            ksrc = lambda a: kn[a][pi][:, (j - 1) * 128:j * 128]
                            kres = [kn[0][pi].res]
                            vsrc, vres = va[pi][:, j - 1, gq, :], va[pi].res
                        else:
                            ksrc = lambda a: kn[a][1 - pi][:, 384:512]
                            kres = [kn[0][1 - pi].res]
                            vsrc, vres = va[1 - pi][:, 3, gq, :], va[1 - pi].res
                    yield
                    sc = nb()
                    for hq in range(4):
                        h = 4 * gq + hq
                        a = 0
                        P.mm(sc[:, hq * 128:(hq + 1) * 128], ksrc(a), qa[h % 2][:, h // 2, j * 128:(j + 1) * 128],
                             hq == 0, False, kres + [qa[0].res, qa[1].res], [sc.res])
                    P.mm(sc[:], ident, mask4[:, which, :, :].rearrange("p a b -> p (a b)"), False, True,
                         [mask4.res] + CR, [sc.res])
                    pt = psb_s[_PC["ctr"] % 2]
                    _PC["ctr"] += 1
                    P.act(pt[:], sc[:], AF.Exp, [sc.res], [pt.res])
                    for hq in range(4):
                        P.mm(oa[:, hq * 65:(hq + 1) * 65], pt[:, hq * 128:(hq + 1) * 128], vsrc,
                             kk == 0 and hq == 0, kk == len(kbs) - 1 and hq == 3, [pt.res, vres], [oa.res])
                dn = den[(2 * j + gq) % 4]
                oav = oa[:, 0:260].rearrange("p (h d) -> p h d", h=4)
                P.tt("dve", dn[:, 0:4], oav[:, :, 64], exps[:, 4 * gq:4 * gq + 4], ALU.add, [oa.res, exps.res], [dn.res])
                P.ts("dve", dn[:, 0:4], dn[:, 0:4], 2.0, None, ALU.mult, ALU.bypass, [dn.res], [dn.res])
                P.recip(dn[:, 4:8], dn[:, 0:4], [dn.res], [dn.res])
                for hq in range(4):
                    h = 4 * gq + hq
                    P.stt("dve", ysb[0][:, j, h * 64:(h + 1) * 64], oa[:, hq * 65:hq * 65 + 64], dn[:, 4 + hq:5 + hq],
                          sg[0][:, j, h * 64:(h + 1) * 64], ALU.mult, ALU.mult, [oa.res, dn.res, sg[0].res], [ysb[0].res])
        for n3, n in ((0, 0),):
            for j in range(4):
                yield
                b = nb()
                bv = b.t[:].bitcast(BF16)
                for c in range(2):
                    P.tr(bv[:, c * 128:(c + 1) * 128], ysb[n3][:, j, c * 128:(c + 1) * 128], ident, [ysb[n3].res] + CR, [b.res])
                P.cp("act" if j % 2 == 0 else "dve", yT[n][:, :, j * 128:(j + 1) * 128],
                     bv[:, 0:256].rearrange("p (c t) -> p c t", c=2), [b.res], [yT[n].res])

    def gen_mem(i):
        pi = i % 2
        uT = uTd[i % 2]
        yield
        w = load_wc(768, 256, 1)
        ybq, ybo = YBHv[(4 * i) // NBP], (4 * i) % NBP
        P.dma("sp", ybt[:], ybq[ybo:ybo + 4].rearrange("j p c -> p j c"), [RYB[i]], [ybt.res], "c_ybt")
        for j in range(4):
            yield
            b = proj_tm(w, 0, 256, uT, j)
            silu2(sg[1][:, j, 0:256], sg[1].res, b[:, 0:256], b.res)
            P.stt("dve", ysb[1][:, j, 0:256], ybt[:, j, :], 0.5, sg[1][:, j, 0:256], ALU.mult, ALU.mult, [ybt.res, sg[1].res], [ysb[1].res])
        yield
        w = load_wc(1024, 512, 1)
        for hm in range(2):
            yield
            b = proj_fm(w, hm * 128, uT)
            r = qknorm(b, ones128, 128, None)
            P.stt("dve", qmn[:, hm, :], b[:], cvx[:, 1:2], r[:], ALU.mult, ALU.mult, [b.res, r.res, cvx.res], [qmn.res])
        for j in range(4):
            yield
            b = proj_tm(w, 256, 256, uT, j)
            silu2(sg[2][:, j, 0:256], sg[2].res, b[:, 0:256], b.res)
        for hm in range(2):
            pts = []
            for mb in range(2):
                yield
                sc = nb()
                P.mm(sc[:], mkT[:, hm, mb * 128:(mb + 1) * 128], qmn[:, hm, :], True, True, [mkT.res, qmn.res], [sc.res])
                pt = psb_m[_PC["mctr"] % 3]
                _PC["mctr"] += 1
                P.act(pt[:], sc[:], AF.Exp, [sc.res], [pt.res])
                pts.append(pt)
            for jp in range(2):
                yield
                om = nb()
                for jj in range(2):
                    j = 2 * jp + jj
                    for mb in range(2):
                        P.mm(om[:, jj * 129:(jj + 1) * 129], pts[mb][:, j * 128:(j + 1) * 128], mva[:, mb, hm, :],
                             jj == 0 and mb == 0, jj == 1 and mb == 1, [pts[mb].res, mva.res], [om.res])
                dn = den[(2 * hm + jp) % 4]
                for jj in range(2):
                    j = 2 * jp + jj
                    P.ts("dve", dn[:, 4 + jj:5 + jj], om[:, jj * 129 + 128:jj * 129 + 129], 2.0, None, ALU.mult, ALU.bypass, [om.res], [dn.res])
                    P.recip(dn[:, jj:jj + 1], dn[:, 4 + jj:5 + jj], [dn.res], [dn.res])
                    P.stt("dve", ysb[2][:, j, hm * 128:(hm + 1) * 128], om[:, jj * 129:jj * 129 + 128], dn[:, jj:jj + 1],
                          sg[2][:, j, hm * 128:(hm + 1) * 128], ALU.mult, ALU.mult, [om.res, dn.res, sg[2].res], [ysb[2].res])
        for n3, n in ((1, 1), (2, 3)):
            for j in range(4):
                yield
                b = nb()
                bv = b.t[:].bitcast(BF16)
                for c in range(2):
                    P.tr(bv[:, c * 128:(c + 1) * 128], ysb[n3][:, j, c * 128:(c + 1) * 128], ident, [ysb[n3].res] + CR, [b.res])
                P.cp("act" if j % 2 == 0 else "dve", yT[n][:, :, j * 128:(j + 1) * 128],
                     bv[:, 0:256].rearrange("p (c t) -> p c t", c=2), [b.res], [yT[n].res])

    def gen_lru(i):
        pi = i % 2
        uT = uTd[i % 2]
        yield
        w = load_wc(1536, 512, 2)
        for c in range(2):
            yield
            b = proj_fm(w, c * 128, uT)
            if i > 0:
                P.cp("dve", cx[:, c, 0:3], cx[:, c, 512:515], [cx.res], [cx.res])
            P.cp("act", cx[:, c, 3:515], b[:], [b.res], [cx.res])
        for c in range(2):
            yield
            b = proj_fm(w, 256 + c * 128, uT)
            silu2(sgc[:, c, :], sgc.res, b)
        for c in range(2):
            q = c % 2
            xcc, xcbb = xc[q], xcb[q]
            tr_, ti_, tm_, tb_ = lt[0], lt[1], lt[2], lt[3]
            P.ts("dve", xcc[:], cx[:, c, 0:512], cv[:, 3 + c:4 + c], cv[:, 19 + c:20 + c], ALU.mult, ALU.add,
                 [cx.res, cv.res], [xcc.res])
            for k in range(1, 4):
                P.stt("dve", xcc[:], cx[:, c, k:k + 512], cv[:, 3 + 4 * k + c:4 + 4 * k + c], xcc[:], ALU.mult, ALU.add,
                      [cx.res, cv.res, xcc.res], [xcc.res])
            P.cp("act", xcbb[:], xcc[:], [xcc.res], [xcbb.res])
            yield
            ba = nb()
            P.mm(ba[:], lwb[:, 0, c, :], xcbb[:], True, True, [lwb.res, xcbb.res], [ba.res])
            bx = nb()
            P.mm(bx[:], lwb[:, 1, c, :], xcbb[:], True, True, [lwb.res, xcbb.res], [bx.res])
            P.act(tr_[:], ba[:], AF.Tanh, [ba.res, cvh.res], [tr_.res], scale=0.5, bias=cvh[:, c:c + 1])
            P.act(ti_[:], bx[:], AF.Tanh, [bx.res, cvh.res], [ti_.res], scale=0.5, bias=cvh[:, 4 + c:5 + c])
            P.ts("dve", tm_[:], tr_[:], cvh[:, 8 + c:9 + c], cvh[:, 8 + c:9 + c], ALU.mult, ALU.add, [tr_.res, cvh.res], [tm_.res])
            P.act(tr_[:], tm_[:], AF.Exp, [tm_.res], [tr_.res])
            P.act(tm_[:], tm_[:], AF.Exp, [tm_.res], [tm_.res], scale=2.0)
            P.act(tm_[:], tm_[:], AF.Sqrt, [tm_.res, epsb.res], [tm_.res], scale=-1.0, bias=epsb[:, 1:2])
            P.stt("dve", tb_[:], ti_[:], 1.0, xcc[:], ALU.add, ALU.mult, [ti_.res, xcc.res], [tb_.res])
            P.stt("dve", tb_[:], tb_[:], 0.5, tm_[:], ALU.mult, ALU.mult, [tb_.res, tm_.res], [tb_.res])
            hcur = ht[q]

            def scan(e, o=hcur[:], a=tr_[:], bb=tb_[:], init=hcar[:, c:c + 1]):
                return e.tensor_tensor_scan(out=o, data0=a, data1=bb, initial=init, op0=ALU.mult, op1=ALU.add)
            P.op("dve", scan, [tr_.res, tb_.res, hcar.res], [hcur.res])
            P.cp("dve", hcar[:, c:c + 1], hcur[:, 511:512], [hcur.res], [hcar.res])
            P.stt("dve", yT[2][:, c, :], hcur[:], 0.5, sgc[:, c, :], ALU.mult, ALU.mult, [hcur.res, sgc.res], [yT[2].res])
        yield

    def run_interleaved(gens):
        gens = list(gens)
        while gens:
            for gq in list(gens):
                try:
                    next(gq)
                except StopIteration:
                    gens.remove(gq)

    XSTEPS = 8 * 5 + 2 * 5 + 2
    xdone = 0
    gx = None
    for i in range(NT):
        P.dma("sp", uTd[i % 2][:], UT[:, :, i * 512:(i + 1) * 512], [RUT[i]], [uTd[i % 2].res], f"c_cut{i % 2}")
        gl = [gen_swa(i), gen_mem(i), gen_lru(i)]
        if gx is None and i % 2 == 1 and i >= 3 and xdone == (i - 3) // 2:
            gx = stage_X(xdone)
            xdone += 1
            xleft = XSTEPS
        first_half = (i % 2 == 1)
        budget = (XSTEPS // 2) if first_half else 10 ** 9
        while gl:
            for gq in list(gl):
                try:
                    next(gq)
                except StopIteration:
                    gl.remove(gq)
            if gx is not None and budget > 0:
                try:
                    next(gx)
                    budget -= 1
                except StopIteration:
                    gx = None
        if gx is not None:
            while budget > 0:
                try:
                    next(gx)
                    budget -= 1
                except StopIteration:
                    gx = None
                    break
        ysh = YTSH[i // 2].rearrange("(h p c) s -> h p c s", h=2, p=128)
        P.dma("pool", ysh[i % 2], yTall[:], [yTall.res], [RYTS[i]], "c_ytall")
        if i % 2 == 1:
            p_ = i // 2
            P.op("pool", lambda e, p_=p_: e.collective_compute("AllGather", ALU.bypass, RG, [YTSH[p_].opt()], [YTSF[p_].opt()]),
                 [RYTS[2 * p_], RYTS[2 * p_ + 1]], [RYTSF[p_]], chan="c_cc", inc=1)
    if gx is not None:
        for _ in gx:
            pass
    for t_ in range(xdone, NTH):
        run_interleaved([stage_X(t_)])


def _consts():
    s = np.arange(128)[:, None]
    t = np.arange(128)[None, :]
    c = np.zeros((128, 8, 128), np.float32)
    c[:, 0] = np.eye(128)
    c[:, 1] = -((s >= t).astype(np.float32))
    c[:, 2] = -1.0
    c[:, 3] = np.where(s < t, 0.0, NEG)
    c[:, 4] = np.where(s <= t, 0.0, NEG)
    c[:, 5] = np.where(s > t, 0.0, NEG)
    bo = np.zeros((128, 128), np.float32)
    bo[:64, :64] = 1.0 / 64
    bo[64:, 64:] = 1.0 / 64
    c[:, 6] = bo
    c[:, 7] = 1.0 / 128
    return c


def _prep(inputs, L):
    f = lambda a: np.ascontiguousarray(np.asarray(a, dtype=np.float32))
    shared, per = {}, [{}, {}]
    wi = f(inputs["w_in"])
    wm = f(inputs["w_mem_kv"])
    shared["w_branch"] = f(inputs["w_branch"])
    shared["w_out"] = f(inputs["w_out"])
    shared["gain_bc"] = f(np.broadcast_to(np.asarray(inputs["norm_gain"])[:, None, :], (L, 128, 1024)))
    shared["mgain_bc"] = f(np.broadcast_to(np.asarray(inputs["mem_norm_gain"])[:, None, :], (L, 128, 1024)))
    shared["consts"] = _consts()
    merge = np.concatenate([wi[:, :, O_MERGE + n * 1024 + dd * 128:O_MERGE + n * 1024 + (dd + 1) * 128]
                            for dd in range(8) for n in range(4)], axis=2)
    cw = np.asarray(inputs["conv_w"])
    for r in range(2):
        p = per[r]
        o = 256 * r
        p["w_sb"] = np.ascontiguousarray(np.concatenate([wi[:, :, O_BQ + o:O_BQ + o + 256], wi[:, :, O_BK + o:O_BK + o + 256],
                                                         wi[:, :, O_BV + o:O_BV + o + 256]], axis=2))
        kg = wi[:, :, O_AK + 64 * r:O_AK + 64 * (r + 1)]
        vg = wi[:, :, O_AV + 64 * r:O_AV + 64 * (r + 1)]
        p["w_c"] = np.ascontiguousarray(np.concatenate([
            wi[:, :, O_AQ + o:O_AQ + o + 256], kg, kg, vg, vg,
            wi[:, :, O_AG + o:O_AG + o + 256], wi[:, :, O_BG + o:O_BG + o + 256],
            wi[:, :, O_MQ + o:O_MQ + o + 256], wi[:, :, O_MG + o:O_MG + o + 256],
            wi[:, :, O_CX + o:O_CX + o + 256], wi[:, :, O_CG + o:O_CG + o + 256], merge], axis=2))
        oo = 256 * (1 - r)
        p["w_mem_kv"] = np.ascontiguousarray(np.concatenate([wm[:, :, o:o + 256], wm[:, :, oo:oo + 256],
                                                             wm[:, :, 512 + o:512 + o + 256], wm[:, :, 512 + oo:512 + oo + 256]], axis=2))
        colv = np.zeros((L, 128, NCV), np.float32)
        colv[:, :, 0] = np.tile(np.asarray(inputs["swa_q_gain"]), (1, 2))
        colv[:, :, 1] = np.tile(np.asarray(inputs["swa_k_gain"]), (1, 2))
        colv[:, :, 2] = np.asarray(inputs["mem_q_gain"])
        for c in range(2):
            gc = 2 * r + c
            for k in range(4):
                colv[:, :, 3 + 4 * k + c] = cw[:, k, gc * 128:(gc + 1) * 128]
            colv[:, :, 19 + c] = np.asarray(inputs["conv_b"])[:, gc * 128:(gc + 1) * 128]
            colv[:, :, 23 + c] = np.asarray(inputs["lru_b_a"])[:, gc * 128:(gc + 1) * 128]
            colv[:, :, 27 + c] = np.asarray(inputs["lru_b_x"])[:, gc * 128:(gc + 1) * 128]
            colv[:, :, 31 + c] = np.asarray(inputs["lru_lambda"])[:, gc * 128:(gc + 1) * 128]
        p["colv"] = colv
        rowv = np.zeros((L, 128, NRV), np.float32)
        rowv[:, :, 0:4] = np.asarray(inputs["swa_sinks"])[:, None, 4 * r:4 * r + 4]
        rowv[:, :, 8:136] = np.asarray(inputs["mem_k_gain"])[:, None, :]
        p["rowv"] = rowv
        lw = np.zeros((L, 2, 4, 128, 128), np.float32)
        for a, nm in enumerate(("lru_w_a", "lru_w_x")):
            wa = np.asarray(inputs[nm])
            for c in range(2):
                gc = 2 * r + c
                lw[:, a, c, 0:64, 0:64] = wa[:, 2 * gc]
                lw[:, a, c, 64:128, 64:128] = wa[:, 2 * gc + 1]
        p["lruw"] = lw
    return shared, per


_NC_CACHE = {}


def run(inputs, S, L, ncores):
    key = (S, L, ncores)
    if key not in _NC_CACHE:
        _NC_CACHE[key] = build(S, L, ncores)
    nc = _NC_CACHE[key]
    shared, per = _prep(inputs, L)
    x = np.asarray(inputs["x"], dtype=np.float32)
    mem = np.asarray(inputs["mem"], dtype=np.float32)
    in_maps = []
    for c in range(ncores):
        b, r = c // 2, c % 2
        m = dict(shared)
        m.update(per[r])
        m["x"] = np.ascontiguousarray(x[b])
        m["x_own"] = np.ascontiguousarray(x[b].reshape(S // 1024, 2, 512, 1024)[:, r].reshape(S // 2, 1024))
        m["rk"] = np.array([[r, 1 - r]], dtype=np.int32)
        m["mem"] = np.ascontiguousarray(mem[b])
        in_maps.append(m)
    res = run_bass_kernel_spmd(nc, in_maps, core_ids=list(range(ncores)))
    return [r["y"] for r in res.results]


def kernel(**inputs):
    outs = run(inputs, 8192, 2, 8)
    return np.stack([_merge(outs[2 * b], outs[2 * b + 1]) for b in range(4)], axis=0).astype(np.float32)


def _merge(y0, y1):
    n = y0.shape[0] // 512
    return np.stack([y0.reshape(n, 512, -1), y1.reshape(n, 512, -1)], axis=1).reshape(2 * y0.shape[0], -1)
```
